# Optimizing a Trainium2 kernel written in Bass

```python
import jax, jax.numpy as jnp
from jax import lax
import numpy as np

D_MODEL = 2048
BATCH = 8
SEQ = 4096
DEPTH = 2

HEAD_DIM = 64
ATTN_WIDTH = D_MODEL // 2
ATTN_HEADS = ATTN_WIDTH // HEAD_DIM
DILATED_BRANCHES = ((128, 1), (512, 4), (2048, 16))
ATTN_BLOCK = 128

SSD_WIDTH = D_MODEL // 2
SSD_HEAD_DIM = 64
SSD_HEADS = SSD_WIDTH // SSD_HEAD_DIM
SSD_GROUPS = 2
SSD_STATE = 128
SSD_CONV = 4
SSD_CHUNK = 128
CONV_CH = SSD_WIDTH + 2 * SSD_GROUPS * SSD_STATE

MIX_WIDTH = ATTN_WIDTH + SSD_WIDTH
IN_PROJ = 3 * ATTN_WIDTH + SSD_WIDTH + CONV_CH + SSD_HEADS
D_FF = 4 * D_MODEL
NORM_EPS = 1e-5

kernel_name = "hybrid_ssd_dilated_alibi_block"


def alibi_slopes(n_heads):
    return jnp.asarray(2.0 ** (-8.0 * (np.arange(n_heads) + 1) / n_heads), dtype=jnp.float32)


def rmsnorm(x, g):
    x32 = x.astype(jnp.float32)
    y = x32 * lax.rsqrt(jnp.mean(x32 * x32, axis=-1, keepdims=True) + NORM_EPS)
    return (y * g.astype(jnp.float32)).astype(x.dtype)


def dilated_branch(q, k, v, slopes, window, dilation):
    b, s, h, dh = q.shape
    L = s // dilation
    nb = -(-L // ATTN_BLOCK)
    Lp = nb * ATTN_BLOCK
    steps = window // dilation

    def to_blocks(t):
        t = t.astype(jnp.float32).reshape(b, L, dilation, h, dh).transpose(0, 2, 3, 1, 4)
        t = jnp.pad(t, ((0, 0), (0, 0), (0, 0), (0, Lp - L), (0, 0)))
        return t.reshape(b, dilation, h, nb, ATTN_BLOCK, dh)

    def with_prev(t):
        prev = jnp.pad(t, ((0, 0), (0, 0), (0, 0), (1, 0), (0, 0), (0, 0)))[:, :, :, :-1]
        return jnp.concatenate([prev, t], axis=4)

    qb, kb, vb = to_blocks(q), to_blocks(k), to_blocks(v)
    kc, vc = with_prev(kb), with_prev(vb)
    scores = jnp.einsum('brhnid,brhnjd->brhnij', qb, kc) * (dh ** -0.5)

    i = jnp.arange(ATTN_BLOCK)[:, None]
    j = jnp.arange(2 * ATTN_BLOCK)[None, :]
    delta = i - j + ATTN_BLOCK
    key_pos = jnp.arange(nb)[:, None, None] * ATTN_BLOCK - ATTN_BLOCK + j
    valid = (delta >= 0) & (delta <= steps) & (key_pos >= 0)
    bias = -slopes[:, None, None, None] * (delta * dilation).astype(jnp.float32)
    scores = jnp.where(valid, scores + bias, -jnp.inf)

    m = jnp.max(scores, axis=-1, keepdims=True)
    p = jnp.exp(scores - m)
    den = jnp.sum(p, axis=-1)
    out = jnp.einsum('brhnij,brhnjd->brhnid', p, vc) / den[..., None]
    lse = m[..., 0] + jnp.log(den)

    out = out.reshape(b, dilation, h, Lp, dh)[:, :, :, :L].transpose(0, 3, 1, 2, 4).reshape(b, s, h, dh)
    lse = lse.reshape(b, dilation, h, Lp)[:, :, :, :L].transpose(0, 3, 1, 2).reshape(b, s, h)
    return out, lse


def dilated_attention(q, k, v, slopes):
    outs, lses = [], []
    for window, dilation in DILATED_BRANCHES:
        o, l = dilated_branch(q, k, v, slopes, window, dilation)
        outs.append(o)
        lses.append(l)
    w = jax.nn.softmax(jnp.stack(lses, axis=0), axis=0)
    out = jnp.sum(w[..., None] * jnp.stack(outs, axis=0), axis=0)
    return out.astype(q.dtype)


def causal_depthwise_conv(u, w, bias):
    out = lax.conv_general_dilated(
        u, w[:, None, :], window_strides=(1,), padding=((SSD_CONV - 1, 0),),
        dimension_numbers=('NWC', 'WIO', 'NWC'), feature_group_count=u.shape[-1])
    return out + bias


def segsum_exp(a):
    cum = jnp.cumsum(a, axis=-1)
    diff = cum[..., :, None] - cum[..., None, :]
    T = a.shape[-1]
    mask = jnp.tril(jnp.ones((T, T), dtype=bool))
    return jnp.exp(jnp.where(mask, diff, -jnp.inf))


def ssd_scan(x, dt, a, b_in, c_in):
    bs, s, h, p = x.shape
    g, n, Q = SSD_GROUPS, SSD_STATE, SSD_CHUNK
    e = h // g
    nc = s // Q
    x = x.astype(jnp.float32)
    X = (x * dt[..., None]).reshape(bs, nc, Q, g, e, p)
    dA = (dt * a).reshape(bs, nc, Q, g, e).transpose(0, 3, 4, 1, 2)
    Bc = b_in.astype(jnp.float32).reshape(bs, nc, Q, g, n)
    Cc = c_in.astype(jnp.float32).reshape(bs, nc, Q, g, n)
    a_cum = jnp.cumsum(dA, axis=-1)

    Lmat = segsum_exp(dA)
    cb = jnp.einsum('bclgn,bcsgn->bgcls', Cc, Bc)
    y_diag = jnp.einsum('bgecls,bcsgep->bclgep', cb[:, :, None] * Lmat, X)

    decay_states = jnp.exp(a_cum[..., -1:] - a_cum)
    states = jnp.einsum('bcsgn,bgecs,bcsgep->bcgepn', Bc, decay_states, X)
    states = jnp.concatenate([jnp.zeros_like(states[:, :1]), states], axis=1)
    decay_chunk = segsum_exp(jnp.pad(a_cum[..., -1], ((0, 0), (0, 0), (0, 0), (1, 0))))
    states = jnp.einsum('bgezc,bcgepn->bzgepn', decay_chunk, states)[:, :-1]

    y_off = jnp.einsum('bclgn,bcgepn,bgecl->bclgep', Cc, states, jnp.exp(a_cum))
    return (y_diag + y_off).reshape(bs, s, h, p)


def hybrid_layer(x, ln1_g, w_in, conv_w, conv_b, dt_bias, a_log, d_skip,
                 attn_norm_g, ssd_norm_g, w_out, ln2_g, w_mlp_in, w_mlp_out, slopes):
    b, s, _ = x.shape
    h = rmsnorm(x, ln1_g)
    proj = h @ w_in
    cuts = [ATTN_WIDTH, 2 * ATTN_WIDTH, 3 * ATTN_WIDTH,
            3 * ATTN_WIDTH + SSD_WIDTH, 3 * ATTN_WIDTH + SSD_WIDTH + CONV_CH]
    q, k, v, z, xbc, dt_raw = jnp.split(proj, cuts, axis=-1)

    q = q.reshape(b, s, ATTN_HEADS, HEAD_DIM)
    k = k.reshape(b, s, ATTN_HEADS, HEAD_DIM)
    v = v.reshape(b, s, ATTN_HEADS, HEAD_DIM)
    attn = dilated_attention(q, k, v, slopes).reshape(b, s, ATTN_WIDTH)
    attn = rmsnorm(attn, attn_norm_g)

    xbc = jax.nn.silu(causal_depthwise_conv(xbc, conv_w, conv_b))
    xs, bm, cm = jnp.split(xbc, [SSD_WIDTH, SSD_WIDTH + SSD_GROUPS * SSD_STATE], axis=-1)
    xs = xs.reshape(b, s, SSD_HEADS, SSD_HEAD_DIM)
    dt = jax.nn.softplus(dt_raw.astype(jnp.float32) + dt_bias.astype(jnp.float32))
    a = -jnp.exp(a_log.astype(jnp.float32))
    y = ssd_scan(xs, dt, a, bm.reshape(b, s, SSD_GROUPS, SSD_STATE),
                 cm.reshape(b, s, SSD_GROUPS, SSD_STATE)).astype(x.dtype)
    y = y + d_skip[:, None] * xs
    y = y.reshape(b, s, SSD_WIDTH) * jax.nn.silu(z)
    y = rmsnorm(y.reshape(b, s, SSD_GROUPS, SSD_WIDTH // SSD_GROUPS),
                ssd_norm_g.reshape(SSD_GROUPS, SSD_WIDTH // SSD_GROUPS)).reshape(b, s, SSD_WIDTH)

    x = x + jnp.concatenate([attn, y], axis=-1) @ w_out

    h = rmsnorm(x, ln2_g)
    x = x + jnp.square(jax.nn.relu(h @ w_mlp_in)) @ w_mlp_out
    return x


def setup_inputs(seed: int = 0) -> dict:
    key = jax.random.key(seed)
    ks = jax.random.split(key, 16)
    f32 = jnp.float32
    dt0 = jnp.exp(jax.random.uniform(ks[5], (DEPTH, SSD_HEADS), f32,
                                     minval=float(np.log(1e-3)), maxval=float(np.log(1e-1))))
    return {
        "x": jax.random.normal(ks[0], (BATCH, SEQ, D_MODEL), f32),
        "ln1_g": 1.0 + 0.02 * jax.random.normal(ks[1], (DEPTH, D_MODEL), f32),
        "w_in": jax.random.normal(ks[2], (DEPTH, D_MODEL, IN_PROJ), f32) * D_MODEL ** -0.5,
        "conv_w": jax.random.normal(ks[3], (DEPTH, SSD_CONV, CONV_CH), f32) * SSD_CONV ** -0.5,
        "conv_b": 0.02 * jax.random.normal(ks[4], (DEPTH, CONV_CH), f32),
        "dt_bias": dt0 + jnp.log(-jnp.expm1(-dt0)),
        "a_log": jnp.log(jax.random.uniform(ks[6], (DEPTH, SSD_HEADS), f32, minval=1.0, maxval=16.0)),
        "d_skip": 1.0 + 0.1 * jax.random.normal(ks[7], (DEPTH, SSD_HEADS), f32),
        "attn_norm_g": 1.0 + 0.02 * jax.random.normal(ks[8], (DEPTH, ATTN_WIDTH), f32),
        "ssd_norm_g": 1.0 + 0.02 * jax.random.normal(ks[9], (DEPTH, SSD_WIDTH), f32),
        "w_out": jax.random.normal(ks[10], (DEPTH, MIX_WIDTH, D_MODEL), f32) * MIX_WIDTH ** -0.5,
        "ln2_g": 1.0 + 0.02 * jax.random.normal(ks[11], (DEPTH, D_MODEL), f32),
        "w_mlp_in": jax.random.normal(ks[12], (DEPTH, D_MODEL, D_FF), f32) * D_MODEL ** -0.5,
        "w_mlp_out": jax.random.normal(ks[13], (DEPTH, D_FF, D_MODEL), f32) * D_FF ** -0.5,
        "final_norm_g": 1.0 + 0.02 * jax.random.normal(ks[14], (D_MODEL,), f32),
    }


def reference(x, ln1_g, w_in, conv_w, conv_b, dt_bias, a_log, d_skip,
              attn_norm_g, ssd_norm_g, w_out, ln2_g, w_mlp_in, w_mlp_out, final_norm_g):
    slopes = alibi_slopes(ATTN_HEADS)
    for l in range(DEPTH):
        x = hybrid_layer(x, ln1_g[l], w_in[l], conv_w[l], conv_b[l], dt_bias[l], a_log[l],
                         d_skip[l], attn_norm_g[l], ssd_norm_g[l], w_out[l], ln2_g[l],
                         w_mlp_in[l], w_mlp_out[l], slopes)
    return rmsnorm(x, final_norm_g)
```

```python
import contextlib
import numpy as np
import concourse.bass as bass
import concourse.mybir as mybir
from concourse.bass_utils import run_bass_kernel_spmd

F32 = mybir.dt.float32
BF16 = mybir.dt.bfloat16
I32 = mybir.dt.int32
AF = mybir.ActivationFunctionType
ALU = mybir.AluOpType

D = 2048
S_LEN = 4096
NL = 2
KC = 16
TT = 1024
NT = S_LEN // TT
EPS = 1e-5
NCOL = 116
DILS = (1, 4, 16)

COMPUTE = ('pe', 'act', 'dve', 'pool')
QUEUES = ('sp', 'act', 'pool')


class T:
    __slots__ = ('name', 'writers', 'readers', 'gen_deps', 'excl')

    def __init__(self, name='', excl=False):
        self.name = name
        self.writers = []
        self.readers = []
        self.gen_deps = []
        self.excl = excl


def TL(name, n):
    return [T('%s%d' % (name, i)) for i in range(n)]


def TX(name):
    return T(name, excl=True)


def TLX(name, n):
    return [T('%s%d' % (name, i), excl=True) for i in range(n)]


class Ins:
    __slots__ = ('eng', 'fn', 'deps', 'signal', 'sigval', 'is_dma', 'dsem', 'dval', 'prev_dma')

    def __init__(self, eng, fn, is_dma=False):
        self.eng = eng
        self.fn = fn
        self.deps = []
        self.signal = False
        self.sigval = None
        self.is_dma = is_dma
        self.dsem = None
        self.dval = None
        self.prev_dma = None


NPOOL = {'sp': 24, 'act': 8, 'pool': 12}


class SemState:
    def __init__(self, nc, st):
        self.csem = {e: st.enter_context(nc.semaphore('c_' + e)) for e in COMPUTE}
        self.dsem = {}
        for q in QUEUES:
            for j in range(NPOOL[q]):
                self.dsem[(q, j)] = st.enter_context(nc.semaphore('d_%s%d' % (q, j)))
        self.ccount = {e: 0 for e in COMPUTE}
        self.ndma = {q: 0 for q in QUEUES}


class Sched:
    def __init__(self, nc, sems, same_engine_sync=True):
        self.nc = nc
        self.sems = sems
        self.same = same_engine_sync
        self.lists = {e: [] for e in ('pe', 'act', 'dve', 'pool', 'sp')}
        self.ndma = dict(sems.ndma)
        self.ndma0 = dict(sems.ndma)
        self.npool = NPOOL
        self.dma_hist = {q: [] for q in QUEUES}

    def _deps_for(self, ins, reads, writes, partial):
        deps = []
        ex = []
        for t in list(reads) + list(writes):
            if t.excl and t not in ex:
                ex.append(t)
        reads = [t for t in reads if not t.excl]
        writes = [t for t in writes if not t.excl]
        for t in ex:
            deps.extend(t.writers)
            t.writers = [ins]
        for t in reads:
            deps.extend(t.writers)
        for t in writes:
            if partial:
                if t.readers or not t.writers:
                    t.gen_deps = list(t.readers) + list(t.writers)
                    t.writers = []
                    t.readers = []
                deps.extend(t.gen_deps)
            else:
                deps.extend(t.readers)
                deps.extend(t.writers)
        for t in reads:
            t.readers.append(ins)
        for t in writes:
            if partial:
                t.writers.append(ins)
            else:
                t.writers = [ins]
                t.readers = []
                t.gen_deps = []
        seen = set()
        out = []
        for d in deps:
            if id(d) in seen or d is ins:
                continue
            seen.add(id(d))
            out.append(d)
        ins.deps = out
        for d in out:
            if d.is_dma:
                continue
            if d.eng == ins.eng and not ins.is_dma and (ins.eng == 'pe' or not self.same):
                continue
            d.signal = True

    def op(self, eng, fn, reads=(), writes=(), partial=False):
        ins = Ins(eng, fn)
        self._deps_for(ins, reads, writes, partial)
        self.lists[eng].append(ins)
        return ins

    def dma(self, q, out, in_, reads=(), writes=(), partial=False, **kw):
        def fn(e, out=out, in_=in_, kw=kw):
            return e.dma_start(out=out, in_=in_, **kw)
        ins = Ins(q, fn, is_dma=True)
        self._deps_for(ins, reads, writes, partial)
        k = self.ndma[q]
        P = self.npool[q]
        ins.dsem = (q, k % P)
        ins.dval = 16 * (k // P + 1)
        hist = self.dma_hist[q]
        if k - self.ndma0[q] >= P:
            ins.prev_dma = hist[k - self.ndma0[q] - P]
        hist.append(ins)
        self.ndma[q] = k + 1
        self.lists[q].append(ins)
        return ins

    def emit(self):
        nc = self.nc
        with contextlib.ExitStack() as st:
            csem = self.sems.csem
            dsem = self.sems.dsem
            for e in COMPUTE:
                c = self.sems.ccount[e]
                for ins in self.lists[e]:
                    if ins.is_dma:
                        continue
                    if ins.signal:
                        c += 1
                        ins.sigval = c
                self.sems.ccount[e] = c
            for q in QUEUES:
                self.sems.ndma[q] = self.ndma[q]
            block = st.enter_context(nc.Block())
            same = self.same

            def run(engname, engobj):
                known = {}

                def wait(key, semh, val):
                    if known.get(key, 0) >= val:
                        return
                    known[key] = val
                    engobj.wait_ge(semh, val)

                for ins in self.lists[engname]:
                    for d in ins.deps:
                        if d.is_dma:
                            wait(d.dsem, dsem[d.dsem], d.dval)
                        else:
                            if d.eng == engname and not ins.is_dma:
                                if engname == 'pe' or not same:
                                    continue
                            wait(d.eng, csem[d.eng], d.sigval)
                    if ins.is_dma:
                        if ins.prev_dma is not None:
                            p = ins.prev_dma
                            wait(p.dsem, dsem[p.dsem], p.dval)
                        bi = ins.fn(engobj)
                        bi.then_inc(dsem[ins.dsem], 16)
                    else:
                        bi = ins.fn(engobj)
                        if ins.signal:
                            bi.then_inc(csem[engname], 1)
                if engname in QUEUES:
                    hist = self.dma_hist[engname]
                    P = self.npool[engname]
                    for ins in hist[-P:]:
                        wait(ins.dsem, dsem[ins.dsem], ins.dval)

            @block.tensor
            def _(e):
                run('pe', e)

            @block.scalar
            def _(e):
                run('act', e)

            @block.vector
            def _(e):
                run('dve', e)

            @block.gpsimd
            def _(e):
                run('pool', e)

            @block.sync
            def _(e):
                run('sp', e)


class Ops:
    def __init__(self, S):
        self.S = S

    def mm(self, out, lhsT, rhs, start, stop, reads, writes):
        return self.S.op('pe', lambda e: e.matmul(out, lhsT=lhsT, rhs=rhs, start=start, stop=stop),
                         reads=reads, writes=writes, partial=True)

    def tr(self, out, in_, ident, reads, writes):
        return self.S.op('pe', lambda e: e.transpose(out, in_, ident), reads=reads, writes=writes, partial=True)

    def act(self, out, in_, func, reads, writes, bias=None, scale=None, partial=False, eng='act'):
        kw = {}
        if bias is not None:
            kw['bias'] = bias
        if scale is not None:
            kw['scale'] = scale
        return self.S.op('act', lambda e: e.activation(out=out, in_=in_, func=func, **kw),
                         reads=reads, writes=writes, partial=partial)

    def tt(self, eng, out, in0, in1, op, reads, writes, partial=False):
        return self.S.op(eng, lambda e: e.tensor_tensor(out=out, in0=in0, in1=in1, op=op),
                         reads=reads, writes=writes, partial=partial)

    def ts(self, eng, out, in0, s1, op0, reads, writes, s2=None, op1=None, partial=False):
        if op1 is None:
            return self.S.op(eng, lambda e: e.tensor_scalar(out=out, in0=in0, scalar1=s1, scalar2=None, op0=op0),
                             reads=reads, writes=writes, partial=partial)
        return self.S.op(eng, lambda e: e.tensor_scalar(out=out, in0=in0, scalar1=s1, scalar2=s2, op0=op0, op1=op1),
                         reads=reads, writes=writes, partial=partial)

    def stt(self, eng, out, in0, scalar, in1, op0, op1, reads, writes, partial=False):
        return self.S.op(eng, lambda e: e.scalar_tensor_tensor(out=out, in0=in0, scalar=scalar, in1=in1, op0=op0, op1=op1),
                         reads=reads, writes=writes, partial=partial)

    def cp(self, eng, out, in_, reads, writes, partial=False):
        if eng == 'act':
            return self.S.op('act', lambda e: e.activation(out=out, in_=in_, func=AF.Copy),
                             reads=reads, writes=writes, partial=partial)
        return self.S.op(eng, lambda e: e.tensor_copy(out=out, in_=in_), reads=reads, writes=writes, partial=partial)

    def recip(self, out, in_, reads, writes, partial=False):
        return self.S.op('dve', lambda e: e.reciprocal(out=out, in_=in_), reads=reads, writes=writes, partial=partial)

    def memset(self, eng, ap, val, writes, partial=False):
        return self.S.op(eng, lambda e: e.memset(ap, val), writes=writes, partial=partial)


class Builder:
    def __init__(self, debug=(), nlayers=NL, phases=None):
        self.debug = set(debug)
        self.nlayers = nlayers
        self.phases = phases
        nc = bass.Bass("TRN2", target_bir_lowering=False)
        self.nc = nc
        dt = nc.dram_tensor

        def nlw(ph):
            if phases is None:
                return NL
            ls = [int(p[1]) for p in phases if p[0] == ph]
            return (max(ls) + 1) if ls else 0
        self.wshapes = {
            "win": [max(nlw('A'), 1), 11 if nlw('A') else 1, 128, KC, 512],
            "wdt": [max(nlw('A'), 1), 128, KC, 16],
            "wout": [max(nlw('D'), 1), 4 if nlw('D') else 1, 128, KC, 512],
            "w1": [max(nlw('E'), 1), 16 if nlw('E') else 1, 128, KC, 512],
            "w2": [max(nlw('E'), 1), 2, 16 if nlw('E') else 1, 128, 32, 128],
        }
        self.xT = dt("xT", [D, S_LEN], F32, kind="ExternalInput").ap()
        self.win = dt("win", self.wshapes["win"], F32, kind="ExternalInput").ap()
        self.wdt = dt("wdt", self.wshapes["wdt"], F32, kind="ExternalInput").ap()
        self.wout = dt("wout", self.wshapes["wout"], F32, kind="ExternalInput").ap()
        self.w1 = dt("w1", self.wshapes["w1"], F32, kind="ExternalInput").ap()
        self.w2 = dt("w2", self.wshapes["w2"], F32, kind="ExternalInput").ap()
        self.cols = dt("cols", [128, NL * NCOL + 16], F32, kind="ExternalInput").ap()
        self.rows = dt("rows", [NL, 2, 16], F32, kind="ExternalInput").ap()
        self.outT = dt("outT", [D, S_LEN], F32, kind="ExternalOutput").ap()

        def scratch(name, shape, dtype):
            kind = "ExternalOutput" if name in self.debug else "Internal"
            return dt(name, shape, dtype, kind=kind).ap()
        self.qT = [scratch("qT%d" % l, [1024, S_LEN], BF16) for l in range(NL)]
        self.kT = [scratch("kT%d" % l, [1024, S_LEN], BF16) for l in range(NL)]
        self.vT = [scratch("vT%d" % l, [1024, S_LEN], BF16) for l in range(NL)]
        self.zT = [scratch("zT%d" % l, [1024, S_LEN], F32) for l in range(NL)]
        self.xbcT = [scratch("xbcT%d" % l, [1536, S_LEN], F32) for l in range(NL)]
        self.dtr = [scratch("dtr%d" % l, [32, 128, 16], F32) for l in range(NL)]
        self.attn_raw = [scratch("attn_raw%d" % l, [1024, S_LEN], F32) for l in range(NL)]
        self.y_raw = [scratch("y_raw%d" % l, [1024, S_LEN], F32) for l in range(NL)]
        self.xa = [scratch("xa%d" % l, [D, S_LEN], F32) for l in range(NL)]
        self.xb = [scratch("xb%d" % l, [D, S_LEN], F32) for l in range(NL)]

    def phase(self, name, fn, *args):
        if self.phases is not None and name not in self.phases:
            return
        nc = self.nc
        with contextlib.ExitStack() as st:
            self.st = st
            self.pname = name
            self.S = Sched(nc, self.sems)
            self.O = Ops(self.S)
            fn(*args)
            self.S.emit()
        nc.all_engine_barrier()

    def sb(self, name, shape, dtype):
        return self.st.enter_context(self.nc.sbuf_tensor(self.pname + "_" + name, shape, dtype))

    def ps(self, name, shape, dtype=F32):
        return self.st.enter_context(self.nc.psum_tensor(self.pname + "_" + name, shape, dtype))

    def bank(self, name, dtype=F32):
        n = 512 if dtype == F32 else 1024
        return self.st.enter_context(self.nc.psum_tensor(self.pname + "_" + name, [128, n], dtype))

    def col(self, l, off, n=1):
        base = l * NCOL + off
        return self.colt[:, base:base + n]

    def load_consts(self, need_ident=False, need_tri=False):
        S, O = self.S, self.O
        self.colt = self.sb("colt", [128, NL * NCOL + 16], F32)
        self.t_col = T('colt')
        S.dma('sp', self.colt[:], self.cols, writes=[self.t_col])
        self.ones_bf = self.sb("ones_bf", [128, 128], BF16)
        self.t_ones = T('ones')
        O.memset('pool', self.ones_bf[:], 1.0, [self.t_ones])
        self.eps_c = self.sb("eps_c", [128, 1], F32)
        self.t_eps = T('eps')
        O.memset('pool', self.eps_c[:], EPS, [self.t_eps])
        if need_ident or need_tri:
            self.io_i = self.sb("io_i", [128, 256], I32)
            self.io_f = self.sb("io_f", [128, 256], F32)
            t_ii = T('io_i')
            self.t_iof = T('io_f')
            io_i = self.io_i
            S.op('pool', lambda e: e.iota(io_i[:].rearrange("p (k c) -> p k c", k=2), pattern=[[-128, 2], [1, 128]],
                                          base=128, channel_multiplier=-1), writes=[t_ii])
            O.cp('dve', self.io_f[:], self.io_i[:], [t_ii], [self.t_iof])
            self.ident = self.sb("ident", [128, 128], BF16)
            self.t_ident = T('ident')
            O.ts('dve', self.ident[:], self.io_f[:, 128:256], 0.0, ALU.is_equal, [self.t_iof], [self.t_ident])
        if need_tri:
            self.tri = self.sb("tri", [128, 128], F32)
            self.t_tri = T('tri')
            O.ts('dve', self.tri[:], self.io_f[:, 128:256], 0.0, ALU.is_ge, [self.t_iof], [self.t_tri])
            self.ones_f = self.sb("ones_f", [128, 128], F32)
            self.t_onesf = T('onesf')
            O.memset('pool', self.ones_f[:], 1.0, [self.t_onesf])
            self.one_c = self.sb("one_c", [128, 1], F32)
            self.t_onec = T('onec')
            O.memset('pool', self.one_c[:], 1.0, [self.t_onec])

    def rms_tile(self, src, nch, c0, ncols, gcol0_l, gcol_off, dst, dst_c0, t_dst, stage, t_stage, sq, t_sq, ps_stat, t_ps,
                 rstd, t_rstd, tag, out_f32_dram=None):
        S, O = self.S, self.O
        nfeat = nch * 128
        S.dma('sp', stage[:, 0:nch, :], src[:, c0:c0 + ncols].rearrange("(kc p) t -> p kc t", p=128), writes=[t_stage])
        for ch in range(nch):
            sqb = sq[ch % 2]
            O.act(sqb[:], stage[:, ch, :], AF.Square, [t_stage], [t_sq[ch % 2]])
            O.mm(ps_stat[:, 0:512], self.ones_bf[:], sqb[:], ch == 0, ch == nch - 1, [self.t_ones, t_sq[ch % 2]], [t_ps])
        O.act(rstd[:], ps_stat[:, 0:512], AF.Sqrt, [t_ps, self.t_eps], [t_rstd], bias=self.eps_c[:, 0:1], scale=1.0 / nfeat)
        O.recip(rstd[:], rstd[:], [t_rstd], [t_rstd])
        for ch in range(nch):
            eng = 'dve'
            gc = self.col(gcol0_l, gcol_off + ch)
            if out_f32_dram is None:
                O.stt(eng, dst[:, ch, dst_c0:dst_c0 + 512], stage[:, ch, :], gc, rstd[:], ALU.mult, ALU.mult,
                      [t_stage, t_rstd, self.t_col], [t_dst], partial=True)
            else:
                O.stt(eng, dst[:, ch, :], stage[:, ch, :], gc, rstd[:], ALU.mult, ALU.mult,
                      [t_stage, t_rstd, self.t_col], [t_dst], partial=True)
        if out_f32_dram is not None:
            S.dma('sp', out_f32_dram[:, c0:c0 + ncols].rearrange("(kc p) t -> p kc t", p=128), dst[:, 0:nch, :], reads=[t_dst])

    def phase_A(self, l, xin):
        S, O = self.S, self.O
        self.load_consts()
        hT = [self.sb("hT%d" % i, [128, KC, TT], BF16) for i in range(2)]
        t_hT = [[T('hT%d_%d' % (i, j)) for j in range(2)] for i in range(2)]
        xst = self.sb("xst", [128, KC, 512], F32)
        t_xst = T('xst')
        sq = [self.sb("sq%d" % i, [128, 512], BF16) for i in range(2)]
        t_sq = TL('sq', 2)
        rstd = self.sb("rstd", [128, 512], F32)
        t_rstd = T('rstd')
        NW = 3
        wb = [self.sb("wb%d" % i, [128, KC, 512], BF16) for i in range(NW)]
        t_wb = TL('wb', NW)
        wdtb = self.sb("wdtb", [128, KC, 16], BF16)
        t_wdt = T('wdt')
        NE = 4
        ev_bf = [self.sb("evb%d" % i, [128, TT], BF16) for i in range(NE)]
        ev_f = [self.sb("evf%d" % i, [128, TT], F32) for i in range(NE)]
        t_evb = TL('evb', NE)
        t_evf = TL('evf', NE)
        dts = self.sb("dts", [128, 8, 16], F32)
        t_dts = T('dts')
        ps_stat = self.bank("ps_stat")
        t_pss = TX('pss')
        NP = 4
        pmm = [self.bank("pmm%d" % i) for i in range(NP)]
        t_pmm = TLX('pmm', NP)
        ps_dt_b = self.bank("ps_dt")
        ps_dt = ps_dt_b[:, 0:128].rearrange("p (a b) -> p a b", a=8)
        t_psdt = TX('psdt')

        S.dma('pool', wdtb[:], self.wdt[l], writes=[t_wdt])

        def ln_tile(tt):
            hb = tt % 2
            for sub in range(2):
                self.rms_tile(xin, KC, tt * TT + sub * 512, 512, l, 0, hT[hb], sub * 512, t_hT[hb][sub], xst, t_xst,
                              sq, t_sq, ps_stat, t_pss, rstd, t_rstd, 'A')

        nw = 0
        wsched = [(tt, mb) for tt in range(NT) for mb in range(11)]

        def issue_w(i):
            if i < len(wsched):
                tt_, mb_ = wsched[i]
                S.dma('pool', wb[i % NW][:], self.win[l, mb_], writes=[t_wb[i % NW]])
        issue_w(0)
        issue_w(1)
        ln_tile(0)
        kps = 0
        kev = 0
        for tt in range(NT):
            hb = tt % 2
            tok0 = tt * TT
            for mb in range(11):
                wi = tt * 11 + mb
                issue_w(wi + 2)
                w = wb[wi % NW]
                tw = t_wb[wi % NW]
                if mb == 5 and tt + 1 < NT:
                    ln_tile(tt + 1)
                for s4 in range(4):
                    m = mb * 4 + s4
                    is_bf = m < 24
                    ei = kev % NE
                    kev += 1
                    ev = ev_bf[ei] if is_bf else ev_f[ei]
                    tev = t_evb[ei] if is_bf else t_evf[ei]
                    for th in range(2):
                        pi = kps % NP
                        kps += 1
                        for kc in range(KC):
                            O.mm(pmm[pi][:, 0:512], w[:, kc, s4 * 128:(s4 + 1) * 128], hT[hb][:, kc, th * 512:(th + 1) * 512],
                                 kc == 0, kc == KC - 1, [tw, t_hT[hb][th]], [t_pmm[pi]])
                        eng = 'act' if (kps % 2 == 0) else 'dve'
                        O.cp(eng, ev[:, th * 512:(th + 1) * 512], pmm[pi][:, 0:512], [t_pmm[pi]], [tev], partial=True)
                    if m < 8:
                        dst = self.qT[l][m * 128:(m + 1) * 128, tok0:tok0 + TT]
                    elif m < 16:
                        dst = self.kT[l][(m - 8) * 128:(m - 7) * 128, tok0:tok0 + TT]
                    elif m < 24:
                        dst = self.vT[l][(m - 16) * 128:(m - 15) * 128, tok0:tok0 + TT]
                    elif m < 32:
                        dst = self.zT[l][(m - 24) * 128:(m - 23) * 128, tok0:tok0 + TT]
                    else:
                        dst = self.xbcT[l][(m - 32) * 128:(m - 31) * 128, tok0:tok0 + TT]
                    S.dma('sp', dst, ev[:], reads=[tev])
            for tb in range(8):
                for kc in range(KC):
                    O.mm(ps_dt[:, tb, :], hT[hb][:, kc, tb * 128:(tb + 1) * 128], wdtb[:, kc, :], kc == 0, kc == KC - 1,
                         [t_wdt, t_hT[hb][tb // 4]], [t_psdt])
            O.cp('dve', dts[:], ps_dt, [t_psdt], [t_dts])
            S.dma('sp', self.dtr[l][tt * 8:(tt + 1) * 8].rearrange("c p h -> p c h"), dts[:], reads=[t_dts])

    def phase_B(self, l):
        S, O = self.S, self.O
        self.load_consts(need_ident=True)
        qc = [self.sb("qc%d" % i, [128, S_LEN], BF16) for i in range(2)]
        kc_ = [self.sb("kc%d" % i, [128, S_LEN], BF16) for i in range(2)]
        vc = [self.sb("vc%d" % i, [128, S_LEN], BF16) for i in range(2)]
        t_q = TL('q', 2)
        t_k = TL('k', 2)
        t_v = TL('v', 2)
        acc = self.sb("acc", [128, 2, S_LEN], F32)
        t_acc = TL('acc', 8)
        Vt = [self.sb("Vt%d" % i, [128, 32, 2, 128], BF16) for i in range(2)]
        t_Vt = [TL('Vt%d_' % i, 32) for i in range(2)]
        Et = [self.sb("Et%d" % i, [128, 2, 256], F32) for i in range(3)]
        t_Et = TL('Et', 3)
        Dc = self.sb("Dc", [128, 256], F32)
        msk = self.sb("msk", [128, 256], F32)
        t_Dc = T('Dc')
        t_msk = T('msk')
        NPB = 3
        pexp = [self.sb("pexp%d" % i, [128, 2, 256], F32) for i in range(NPB)]
        Pb = [self.sb("Pb%d" % i, [128, 2, 256], BF16) for i in range(NPB)]
        t_pexp = TL('pexp', NPB)
        t_Pb = TL('Pb', NPB)
        rec = self.sb("rec", [128, S_LEN], F32)
        t_rec = T('rec')
        ob = self.sb("ob", [128, S_LEN], F32)
        t_ob = T('ob')
        ps_s = [self.ps("ps_s%d" % i, [128, 2, 512]) for i in range(2)]
        t_pss = [TLX('pss%d_' % i, 2) for i in range(2)]
        ps_o_b = [self.bank("ps_o%d" % i) for i in range(2)]
        ps_o = [x[:, 0:256].rearrange("p (a b) -> p a b", a=2) for x in ps_o_b]
        t_pso = TLX('pso', 2)
        ps_t_b = [self.bank("ps_t%d" % i, BF16) for i in range(2)]
        ps_t = [x[:, 0:512].rearrange("p (a b) -> p a b", a=4) for x in ps_t_b]
        t_pst = TLX('pst', 2)

        O.ts('dve', Dc[:], self.io_f[:], 0.0, ALU.max, [self.t_iof], [t_Dc], s2=128.0, op1=ALU.min)
        O.ts('dve', msk[:], self.io_f[:], 0.0, ALU.is_ge, [self.t_iof], [t_msk])
        O.stt('dve', msk[:], self.io_f[:], 128.0, msk[:], ALU.is_le, ALU.mult, [self.t_iof, t_msk], [t_msk])
        import os
        KB = os.environ.get("KB", "")
        for i in range(2):
            if 'noms' in KB:
                break
            O.memset('pool', Vt[i][:, :, 0, 64:128], 1.0, t_Vt[i], partial=True)
            O.memset('pool', Vt[i][:, :, 1, 0:64], 1.0, t_Vt[i], partial=True)

        def load_pair(hp):
            b = hp % 2
            r0 = hp * 128
            S.dma('sp', qc[b][:], self.qT[l][r0:r0 + 128, :], writes=[t_q[b]])
            S.dma('sp', kc_[b][:], self.kT[l][r0:r0 + 128, :], writes=[t_k[b]])
            S.dma('sp', vc[b][:], self.vT[l][r0:r0 + 128, :], writes=[t_v[b]])
        load_pair(0)
        ku = 0
        kt = 0
        kvt = 0
        import os
        KB = os.environ.get("KB", "")
        nhp = int(os.environ.get("KB_HP", "8"))
        nbr = int(os.environ.get("KB_BR", "3"))
        for hp in range(nhp):
            pb_ = hp % 2
            if hp + 1 < 8:
                load_pair(hp + 1)
            Q, K, V = qc[pb_], kc_[pb_], vc[pb_]
            for b in range(3):
                if 'noE' in KB:
                    break
                for hh in range(2):
                    h = 2 * hp + hh
                    slope = 2.0 ** (-8.0 * (h + 1) / 16.0)
                    O.act(Et[b][:, hh, :], Dc[:], AF.Exp, [t_Dc], [t_Et[b]], scale=-slope * DILS[b], partial=True)
                O.tt('dve', Et[b][:], Et[b][:], msk[:].unsqueeze(1).to_broadcast([128, 2, 256]), ALU.mult,
                     [t_Et[b], t_msk], [t_Et[b]])
            for b in range(nbr):
                dil = DILS[b]
                nblk = 32 // dil
                vb = kvt % 2
                kvt += 1
                VT = Vt[vb]
                tVT = t_Vt[vb]

                def tslice(r, n):
                    base = r + dil * 128 * n
                    return slice(base, base + dil * 127 + 1, dil)
                for kb0 in range(0, 32, 4):
                    if 'noV' in KB:
                        break
                    if 'V1' in KB and kb0 >= 4:
                        break
                    if 'V2' in KB and kb0 >= 8:
                        break
                    if 'V3' in KB and kb0 >= 12:
                        break
                    pt = ps_t[kt % 2]
                    tpt = t_pst[kt % 2]
                    kt += 1
                    for i4 in range(4):
                        kb = kb0 + i4
                        r, n = kb // nblk, kb % nblk
                        O.tr(pt[:, i4, :], V[:, tslice(r, n)], self.ident[:], [t_v[pb_], self.t_ident], [tpt])
                    if 'nocp' in KB:
                        continue
                    O.cp('act', VT[:, kb0:kb0 + 4, 0, 0:64], pt[:, :, 0:64], [tpt], tVT[kb0:kb0 + 4], partial=True)
                    O.cp('dve', VT[:, kb0:kb0 + 4, 1, 64:128], pt[:, :, 64:128], [tpt], tVT[kb0:kb0 + 4], partial=True)
                for kb in range(32):
                    if 'nounit' in KB:
                        break
                    r, n = kb // nblk, kb % nblk
                    qs = tslice(r, n)
                    si = ku % 2
                    pi = ku % NPB
                    ku += 1
                    pss, tps = ps_s[si], t_pss[si]
                    lo = 0 if n > 0 else 128
                    for hh in range(2):
                        pr = slice(hh * 64, hh * 64 + 64)
                        if n > 0:
                            O.mm(pss[:, hh, 0:128], K[pr, tslice(r, n - 1)], Q[pr, qs], True, True, [t_k[pb_], t_q[pb_]], [tps[hh]])
                        O.mm(pss[:, hh, 128:256], K[pr, qs], Q[pr, qs], True, True, [t_k[pb_], t_q[pb_]], [tps[hh]])
                    if 'noexp' in KB:
                        continue
                    O.act(pexp[pi][:, :, lo:256], pss[:, :, lo:256], AF.Exp, tps, [t_pexp[pi]], scale=0.125)
                    O.tt('pool', Pb[pi][:, :, lo:256], pexp[pi][:, :, lo:256], Et[b][:, :, lo:256], ALU.mult,
                         [t_pexp[pi], t_Et[b]], [t_Pb[pi]])
                    if 'nopv' in KB:
                        continue
                    pso, tpo = ps_o[si], t_pso[si]
                    for hh in range(2):
                        if n > 0:
                            O.mm(pso[:, hh, :], VT[:, kb - 1, hh, :], Pb[pi][:, hh, 0:128], True, False, [tVT[kb - 1], t_Pb[pi]], [tpo])
                        O.mm(pso[:, hh, :], VT[:, kb, hh, :], Pb[pi][:, hh, 128:256], n == 0, True, [tVT[kb], t_Pb[pi]], [tpo])
                    if 'noacc' in KB:
                        continue
                    base = r + dil * 128 * n
                    blks = sorted(set([base // 512, (base + dil * 127) // 512]))
                    blks = list(range(blks[0], blks[-1] + 1))
                    tacc = [t_acc[x] for x in blks]
                    if b == 0:
                        O.cp('dve', acc[:, :, qs], pso, [tpo], tacc, partial=True)
                    else:
                        O.tt('dve', acc[:, :, qs], acc[:, :, qs], pso, ALU.add, [tpo] + tacc, tacc)
            if 'nofin' in KB:
                continue
            O.recip(rec[0:64, :], acc[64:128, 0, :], t_acc, [t_rec], partial=True)
            O.recip(rec[64:128, :], acc[0:64, 1, :], t_acc, [t_rec], partial=True)
            O.tt('pool', ob[0:64, :], acc[0:64, 0, :], rec[0:64, :], ALU.mult, t_acc + [t_rec], [t_ob], partial=True)
            O.tt('pool', ob[64:128, :], acc[64:128, 1, :], rec[64:128, :], ALU.mult, t_acc + [t_rec], [t_ob], partial=True)
            S.dma('sp', self.attn_raw[l][hp * 128:(hp + 1) * 128, :], ob[:], reads=[t_ob])

    def phase_C(self, l):
        S, O = self.S, self.O
        self.load_consts(need_ident=True, need_tri=True)
        tri, ones_f = self.tri, self.ones_f
        dtb = self.sb("dtb", [128, 16], F32)
        aneg = self.sb("aneg", [128, 16], F32)
        t_dtb, t_an = T('dtb'), T('aneg')
        S.dma('sp', dtb[:], self.rows[l, 0].partition_broadcast(128), writes=[t_dtb])
        S.dma('sp', aneg[:], self.rows[l, 1].partition_broadcast(128), writes=[t_an])
        O.act(aneg[:], aneg[:], AF.Exp, [t_an], [t_an])
        O.ts('dve', aneg[:], aneg[:], -1.0, ALU.mult, [t_an], [t_an])
        dt_all = self.sb("dt_all", [128, 32, 16], F32)
        dA_all = self.sb("dA_all", [128, 32, 16], F32)
        t_dt, t_dA = T('dt'), T('dA')
        S.dma('sp', dt_all[:], self.dtr[l].rearrange("c p h -> p c h"), writes=[t_dt])
        O.tt('dve', dt_all[:], dt_all[:], dtb[:].unsqueeze(1).to_broadcast([128, 32, 16]), ALU.add, [t_dt, t_dtb], [t_dt])
        O.act(dt_all[:], dt_all[:], AF.Exp, [t_dt], [t_dt])
        O.act(dt_all[:], dt_all[:], AF.Ln, [t_dt, self.t_onec], [t_dt], bias=self.one_c[:, 0:1])
        O.tt('dve', dA_all[:], dt_all[:], aneg[:].unsqueeze(1).to_broadcast([128, 32, 16]), ALU.mult, [t_dt, t_an], [t_dA])

        xr = self.sb("xr", [128, 12, 515], F32)
        t_xr = T('xr')
        cv = self.sb("cv", [128, 12, 512], F32)
        t_cv = TL('cv', 12)
        xs_bf = self.sb("xs_bf", [128, 8, 512], BF16)
        t_xsb = T('xsb')
        bcf = self.sb("bcf", [128, 4, 512], BF16)
        t_bcf = T('bcf')
        zt = self.sb("zt", [128, 8, 512], F32)
        t_zt = T('zt')
        yst = self.sb("yst", [128, 8, 512], F32)
        t_yst = T('yst')
        cbm = self.sb("cbm", [128, 2, 128], BF16)
        t_cbm = T('cbm')
        Btok = self.sb("Btok", [128, 2, 128], BF16)
        t_Btok = T('Btok')
        Xdt = self.sb("Xdt", [128, 16, 64], BF16)
        t_Xdt = T('Xdt')
        Xw = self.sb("Xw", [128, 16, 64], BF16)
        t_Xw = T('Xw')
        nacs = self.sb("nacs", [128, 16], F32)
        t_nacs = T('nacs')
        dAtri = self.sb("dAtri", [128, 16, 128], F32)
        t_dAtri = TL('dAtri', 4)
        E2 = self.sb("E2", [128, 16, 128], F32)
        t_E2 = TL('E2', 4)
        Lraw = self.sb("Lraw", [128, 16, 128], F32)
        t_Lraw = TL('Lraw', 16)
        MT = self.sb("MT", [128, 16, 128], BF16)
        t_MT = TL('MT', 16)
        Cp = self.sb("Cp", [128, 16, 128], BF16)
        t_Cp = TL('Cp', 4)
        tmp = self.sb("tmp", [128, 8, 128], F32)
        t_tmp = TL('tmp', 2)
        R = self.sb("R", [128, 16, 64], F32)
        t_R = T('R')
        Rbf = self.sb("Rbf", [128, 16, 64], BF16)
        t_Rbf = T('Rbf')

        ps_misc = self.bank("ps_misc")
        ps_cb = ps_misc[:, 0:256].rearrange("p (g l) -> p g l", g=2)
        ps_a = ps_misc[:, 256:272]
        t_pcb = TX('pmisc')
        t_pa = t_pcb
        ps_tx_b = self.bank("ps_tx", BF16)
        ps_tx = ps_tx_b[:, :].rearrange("p (j c) -> p j c", j=8)
        t_ptx = TX('ptx')
        ps_tb_b = self.bank("ps_tb", BF16)
        ps_tb = ps_tb_b[:, 0:256].rearrange("p (j c) -> p j c", j=2)
        t_ptb = TX('ptb')
        ps_bc_b = [self.bank("ps_bc%d" % i) for i in range(2)]
        ps_bc = [x[:, :].rearrange("p (h l) -> p h l", h=4) for x in ps_bc_b]
        t_pbc = TLX('pbc', 2)
        ps_y = self.ps("ps_y", [128, 8, 128])
        t_py = TLX('py', 2)
        ps_st = self.bank("ps_st")
        t_pst = TX('pst')

        O.memset('pool', xr[:, :, 0:3], 0.0, [t_xr])
        cw0 = 48
        kbc = 0
        for t8 in range(8):
            c0 = t8 * 512
            if t8 == 0:
                S.dma('sp', xr[:, :, 3:515], self.xbcT[l][:, 0:512].rearrange("(c p) t -> p c t", p=128), writes=[t_xr], partial=True)
            else:
                S.dma('sp', xr[:, :, :], self.xbcT[l][:, c0 - 3:c0 + 512].rearrange("(c p) t -> p c t", p=128), writes=[t_xr])
            S.dma('sp', zt[:], self.zT[l][:, c0:c0 + 512].rearrange("(c p) t -> p c t", p=128), writes=[t_zt])
            for ch in range(12):
                eng = 'dve'
                O.ts(eng, cv[:, ch, :], xr[:, ch, 0:512], self.col(l, cw0 + ch * 4 + 0), ALU.mult, [t_xr, self.t_col], [t_cv[ch]],
                     s2=self.col(l, 96 + ch), op1=ALU.add)
                for k in range(1, 4):
                    O.stt(eng, cv[:, ch, :], xr[:, ch, k:k + 512], self.col(l, cw0 + ch * 4 + k), cv[:, ch, :], ALU.mult, ALU.add,
                          [t_xr, t_cv[ch], self.t_col], [t_cv[ch]])
            O.act(cv[:, 0:8, :], cv[:, 0:8, :], AF.Silu, t_cv[0:8], t_cv[0:8])
            O.act(bcf[:], cv[:, 8:12, :], AF.Silu, t_cv[8:12], [t_bcf])
            O.cp('pool', xs_bf[:], cv[:, 0:8, :], t_cv[0:8], [t_xsb])
            O.act(zt[:], zt[:], AF.Silu, [t_zt], [t_zt])
            for j in range(8):
                eng = 'dve' if j % 2 == 0 else 'pool'
                O.ts(eng, cv[:, j, :], cv[:, j, :], self.col(l, 108 + j), ALU.mult, [t_cv[j], self.t_col], [t_cv[j]])
            for c in range(4):
                cc = t8 * 4 + c
                cs = slice(c * 128, (c + 1) * 128)
                for g in range(2):
                    O.mm(ps_cb[:, g, :], bcf[:, g, cs], bcf[:, 2 + g, cs], True, True, [t_bcf], [t_pcb])
                O.tt('dve', cbm[:], ps_cb, tri[:].unsqueeze(1).to_broadcast([128, 2, 128]), ALU.mult, [t_pcb, self.t_tri], [t_cbm])
                for g in range(2):
                    O.tr(ps_tb[:, g, :], bcf[:, g, cs], self.ident[:], [t_bcf, self.t_ident], [t_ptb])
                O.cp('act', Btok[:], ps_tb, [t_ptb], [t_Btok])
                for j in range(8):
                    O.tr(ps_tx[:, j, :], xs_bf[:, j, cs], self.ident[:], [t_xsb, self.t_ident], [t_ptx])
                O.tt('dve', Xdt[:], ps_tx_b[:, :].rearrange("p (h d) -> p h d", h=16),
                     dt_all[:, cc, :].unsqueeze(2).to_broadcast([128, 16, 64]), ALU.mult, [t_ptx, t_dt], [t_Xdt])
                O.mm(ps_a, tri[:], dA_all[:, cc, :], True, True, [self.t_tri, t_dA], [t_pa])
                O.ts('dve', nacs[:], ps_a, -1.0, ALU.mult, [t_pa], [t_nacs])
                for bq in range(4):
                    hs = slice(4 * bq, 4 * bq + 4)
                    g = bq // 2
                    O.tt('pool', dAtri[:, hs, :], tri[:].unsqueeze(1).to_broadcast([128, 4, 128]),
                         dA_all[:, cc, hs].unsqueeze(2).to_broadcast([128, 4, 128]), ALU.mult, [self.t_tri, t_dA], [t_dAtri[bq]])
                    pbc, tpbc = ps_bc[kbc % 2], t_pbc[kbc % 2]
                    kbc += 1
                    O.mm(ps_bc_b[(kbc - 1) % 2][:, :], ones_f[:], dAtri[:, hs, :].rearrange("p h l -> p (h l)"), True, True,
                         [self.t_onesf, t_dAtri[bq]], [tpbc])
                    O.act(E2[:, hs, :], pbc, AF.Exp, [tpbc], [t_E2[bq]])
                    for hl in range(4):
                        h = 4 * bq + hl
                        O.ts('dve', Lraw[:, h, :], pbc[:, hl, :], nacs[:, h:h + 1], ALU.add, [tpbc, t_nacs], [t_Lraw[h]],
                             s2=0.0, op1=ALU.min)
                    O.act(Lraw[:, hs, :], Lraw[:, hs, :], AF.Exp, t_Lraw[4 * bq:4 * bq + 4], t_Lraw[4 * bq:4 * bq + 4])
                    O.tt('pool', MT[:, hs, :], Lraw[:, hs, :], cbm[:, g, :].unsqueeze(1).to_broadcast([128, 4, 128]), ALU.mult,
                         t_Lraw[4 * bq:4 * bq + 4] + [t_cbm], t_MT[4 * bq:4 * bq + 4])
                    O.tt('pool', Cp[:, hs, :], E2[:, hs, :], bcf[:, 2 + g, cs].unsqueeze(1).to_broadcast([128, 4, 128]), ALU.mult,
                         [t_E2[bq], t_bcf], [t_Cp[bq]])
                O.tt('pool', Xw[:], Xdt[:], Lraw[:, :, 127:128].to_broadcast([128, 16, 64]), ALU.mult, [t_Xdt] + t_Lraw, [t_Xw])
                for j in range(8):
                    for hh in range(2):
                        h = 2 * j + hh
                        o = ps_y[hh * 64:(hh + 1) * 64, j, :]
                        O.mm(o, Xdt[:, h, :], MT[:, h, :], True, cc == 0, [t_Xdt, t_MT[h]], [t_py[j // 4]])
                        if cc > 0:
                            O.mm(o, Rbf[:, h, :], Cp[:, h, :], False, True, [t_Rbf, t_Cp[h // 4]], [t_py[j // 4]])
                for yb in range(2):
                    js = slice(4 * yb, 4 * yb + 4)
                    O.tt('dve', tmp[:, js, :], ps_y[:, js, :], cv[:, js, cs], ALU.add, [t_py[yb]] + t_cv[4 * yb:4 * yb + 4], [t_tmp[yb]])
                    O.tt('pool', yst[:, js, cs], tmp[:, js, :], zt[:, js, cs], ALU.mult, [t_tmp[yb], t_zt], [t_yst], partial=True)
                if cc < 31:
                    for g in range(2):
                        gs = slice(8 * g, 8 * g + 8)
                        O.mm(ps_st[:, :], Btok[:, g, :], Xw[:, gs, :].rearrange("p h d -> p (h d)"), True, True,
                             [t_Btok, t_Xw], [t_pst])
                        stv = ps_st[:, :].rearrange("p (h d) -> p h d", h=8)
                        if cc == 0:
                            O.cp('dve', R[:, gs, :], stv, [t_pst], [t_R], partial=True)
                        else:
                            O.tt('dve', R[:, gs, :], R[:, gs, :], E2[:, gs, 127:128].to_broadcast([128, 8, 64]), ALU.mult,
                                 [t_R] + t_E2, [t_R])
                            O.tt('dve', R[:, gs, :], R[:, gs, :], stv, ALU.add, [t_R, t_pst], [t_R])
                    O.cp('act', Rbf[:], R[:], [t_R], [t_Rbf])
            S.dma('sp', self.y_raw[l][:, c0:c0 + 512].rearrange("(c p) t -> p c t", p=128), yst[:], reads=[t_yst])

    def phase_D(self, l, xin):
        S, O = self.S, self.O
        self.load_consts()
        mixT = [self.sb("mixT%d" % i, [128, KC, TT], BF16) for i in range(2)]
        t_mix = [[T('mix%d_%d' % (i, j)) for j in range(2)] for i in range(2)]
        stg = self.sb("stg", [128, 8, 512], F32)
        t_stg = T('stg')
        sq = [self.sb("sq%d" % i, [128, 512], BF16) for i in range(2)]
        t_sq = TL('sq', 2)
        rstd = self.sb("rstd", [128, 512], F32)
        t_rstd = T('rstd')
        NW = 3
        wb = [self.sb("wb%d" % i, [128, KC, 512], BF16) for i in range(NW)]
        t_wb = TL('wb', NW)
        NE = 4
        xres = [self.sb("xres%d" % i, [128, TT], F32) for i in range(NE)]
        t_xres = TL('xres', NE)
        ps_stat = self.bank("ps_stat")
        t_pss = TX('pss')
        NP = 4
        pmm = [self.bank("pmm%d" % i) for i in range(NP)]
        t_pmm = TLX('pmm', NP)

        class View:
            def __init__(self, t, off):
                self.t, self.off = t, off

            def __getitem__(self, key):
                p, ch, c = key
                return self.t[p, ch + self.off, c]

        def norm_tile(tt):
            hb = tt % 2
            for sub in range(2):
                c0 = tt * TT + sub * 512
                self.rms_tile(self.attn_raw[l], 8, c0, 512, l, 32, View(mixT[hb], 0), sub * 512, t_mix[hb][sub], stg, t_stg,
                              sq, t_sq, ps_stat, t_pss, rstd, t_rstd, 'D')
                for g in range(2):
                    self.rms_tile(self.y_raw[l][g * 512:(g + 1) * 512, :], 4, c0, 512, l, 40 + 4 * g, View(mixT[hb], 8 + 4 * g),
                                  sub * 512, t_mix[hb][sub], stg, t_stg, sq, t_sq, ps_stat, t_pss, rstd, t_rstd, 'D')

        wsched = [(tt, mb) for tt in range(NT) for mb in range(4)]

        def issue_w(i):
            if i < len(wsched):
                tt_, mb_ = wsched[i]
                S.dma('pool', wb[i % NW][:], self.wout[l, mb_], writes=[t_wb[i % NW]])
        issue_w(0)
        issue_w(1)
        norm_tile(0)
        kps = 0
        kev = 0
        for tt in range(NT):
            hb = tt % 2
            tok0 = tt * TT
            for mb in range(4):
                wi = tt * 4 + mb
                issue_w(wi + 2)
                w, tw = wb[wi % NW], t_wb[wi % NW]
                if mb == 1 and tt + 1 < NT:
                    norm_tile(tt + 1)
                for s4 in range(4):
                    m = mb * 4 + s4
                    ei = kev % NE
                    kev += 1
                    xr_, txr = xres[ei], t_xres[ei]
                    S.dma('sp', xr_[:], xin[m * 128:(m + 1) * 128, tok0:tok0 + TT], writes=[txr])
                    for th in range(2):
                        pi = kps % NP
                        kps += 1
                        for kc in range(KC):
                            O.mm(pmm[pi][:, 0:512], w[:, kc, s4 * 128:(s4 + 1) * 128], mixT[hb][:, kc, th * 512:(th + 1) * 512],
                                 kc == 0, kc == KC - 1, [tw, t_mix[hb][th]], [t_pmm[pi]])
                        O.tt('dve', xr_[:, th * 512:(th + 1) * 512], xr_[:, th * 512:(th + 1) * 512], pmm[pi][:, 0:512], ALU.add,
                             [t_pmm[pi], txr], [txr])
                    S.dma('sp', self.xa[l][m * 128:(m + 1) * 128, tok0:tok0 + TT], xr_[:], reads=[txr])

    def phase_E(self, l):
        S, O = self.S, self.O
        self.load_consts()
        xa, xb = self.xa[l], self.xb[l]
        hT = self.sb("h2T", [128, KC, TT], BF16)
        t_hT = TL('h2T', 2)
        aT = self.sb("aT", [128, 32, TT], BF16)
        t_aT = [[T('aT%d_%d' % (i, j)) for j in range(2)] for i in range(32)]
        xst = self.sb("xst", [128, KC, 512], F32)
        t_xst = T('xst')
        sq = [self.sb("sq%d" % i, [128, 512], BF16) for i in range(2)]
        t_sq = TL('sq', 2)
        rstd = self.sb("rstd", [128, 512], F32)
        t_rstd = T('rstd')
        NW = 3
        wb = [self.sb("wb%d" % i, [128, 8192], BF16) for i in range(NW)]
        t_wb = TL('wb', NW)
        NR = 3
        rl = [self.sb("rl%d" % i, [128, 512], F32) for i in range(NR)]
        t_rl = TL('rl', NR)
        NE = 4
        xres = [self.sb("xres%d" % i, [128, TT], F32) for i in range(NE)]
        t_xres = TL('xres', NE)
        ps_stat = self.bank("ps_stat")
        t_pss = TX('pss')
        NP = 6
        pmm = [self.bank("pmm%d" % i) for i in range(NP)]
        t_pmm = TLX('pmm', NP)

        wsched = []
        for tt in range(NT):
            for half in range(2):
                for b in range(8):
                    wsched.append(('w1', half * 8 + b))
                for j in range(16):
                    wsched.append(('w2', half, j))

        def issue_w(i):
            if i < len(wsched):
                e = wsched[i]
                buf = wb[i % NW]
                if e[0] == 'w1':
                    S.dma('pool', buf[:].rearrange("p (k c) -> p k c", k=KC), self.w1[l, e[1]], writes=[t_wb[i % NW]])
                else:
                    S.dma('pool', buf[:, 0:4096].rearrange("p (k c) -> p k c", k=32), self.w2[l, e[1], e[2]], writes=[t_wb[i % NW]])
        issue_w(0)
        issue_w(1)
        wi = 0
        kps = 0
        kev = 0
        krl = 0
        for tt in range(NT):
            tok0 = tt * TT
            for sub in range(2):
                self.rms_tile(xa, KC, tok0 + sub * 512, 512, l, 16, hT, sub * 512, t_hT[sub], xst, t_xst, sq, t_sq, ps_stat, t_pss,
                              rstd, t_rstd, 'E')
            for half in range(2):
                for b in range(8):
                    issue_w(wi + 2)
                    w = wb[wi % NW][:].rearrange("p (k c) -> p k c", k=KC)
                    tw = t_wb[wi % NW]
                    wi += 1
                    for s4 in range(4):
                        fc = b * 4 + s4
                        for th in range(2):
                            pi = kps % NP
                            kps += 1
                            for kc in range(KC):
                                O.mm(pmm[pi][:, 0:512], w[:, kc, s4 * 128:(s4 + 1) * 128], hT[:, kc, th * 512:(th + 1) * 512],
                                     kc == 0, kc == KC - 1, [tw, t_hT[th]], [t_pmm[pi]])
                            ri = krl % NR
                            krl += 1
                            O.act(rl[ri][:], pmm[pi][:, 0:512], AF.Relu, [t_pmm[pi]], [t_rl[ri]])
                            eng = 'dve' if krl % 2 == 0 else 'pool'
                            O.tt(eng, aT[:, fc, th * 512:(th + 1) * 512], rl[ri][:], rl[ri][:], ALU.mult, [t_rl[ri]], [t_aT[fc][th]])
                for j in range(16):
                    issue_w(wi + 2)
                    w = wb[wi % NW][:, 0:4096].rearrange("p (k c) -> p k c", k=32)
                    tw = t_wb[wi % NW]
                    wi += 1
                    ei = kev % NE
                    kev += 1
                    xr_, txr = xres[ei], t_xres[ei]
                    src = xa if half == 0 else xb
                    S.dma('sp', xr_[:], src[j * 128:(j + 1) * 128, tok0:tok0 + TT], reads=[], writes=[txr])
                    for th in range(2):
                        pi = kps % NP
                        kps += 1
                        for kc in range(32):
                            O.mm(pmm[pi][:, 0:512], w[:, kc, :], aT[:, kc, th * 512:(th + 1) * 512], kc == 0, kc == 31,
                                 [tw, t_aT[kc][th]], [t_pmm[pi]])
                        O.tt('dve', xr_[:, th * 512:(th + 1) * 512], xr_[:, th * 512:(th + 1) * 512], pmm[pi][:, 0:512], ALU.add,
                             [t_pmm[pi], txr], [txr])
                    S.dma('sp', xb[j * 128:(j + 1) * 128, tok0:tok0 + TT], xr_[:], reads=[txr])

    def phase_F(self, xin):
        S, O = self.S, self.O
        self.load_consts()
        xst = [self.sb("xst%d" % i, [128, KC, 512], F32) for i in range(2)]
        t_xst = TL('xst', 2)
        sq = [self.sb("sq%d" % i, [128, 512], BF16) for i in range(2)]
        t_sq = TL('sq', 2)
        rstd = self.sb("rstd", [128, 512], F32)
        t_rstd = T('rstd')
        ps_stat = [self.bank("ps_stat%d" % i) for i in range(2)]
        t_pss = TLX('pss', 2)
        ost = [self.sb("ost%d" % i, [128, KC, 512], F32) for i in range(2)]
        t_ost = TL('ost', 2)
        for i in range(8):
            self.rms_tile(xin, KC, i * 512, 512, NL, 0, ost[i % 2], 0, t_ost[i % 2], xst[i % 2], t_xst[i % 2], sq, t_sq,
                          ps_stat[i % 2], t_pss[i % 2], rstd, t_rstd, 'F', out_f32_dram=self.outT)

    def build(self):
        x = self.xT
        with contextlib.ExitStack() as gst:
            self.sems = SemState(self.nc, gst)
            for l in range(self.nlayers):
                self.phase('A%d' % l, self.phase_A, l, x)
                self.phase('B%d' % l, self.phase_B, l)
                self.phase('C%d' % l, self.phase_C, l)
                self.phase('D%d' % l, self.phase_D, l, x)
                self.phase('E%d' % l, self.phase_E, l)
                x = self.xb[l]
            self.phase('F', self.phase_F, x)
        return self.nc


def tile_w(w, mbw):
    K, M = w.shape
    return np.ascontiguousarray(w.reshape(K // 128, 128, M // mbw, mbw).transpose(2, 1, 0, 3))


def colize(v):
    return v.reshape(-1, 128).T


def prep_shared(inp):
    f = lambda a: np.asarray(a, dtype=np.float32)
    win = np.stack([tile_w(f(inp["w_in"][l])[:, :5632], 512) for l in range(NL)])
    wdt = np.stack([np.ascontiguousarray(f(inp["w_in"][l])[:, 5632:].reshape(KC, 128, 16).transpose(1, 0, 2)) for l in range(NL)])
    wout = np.stack([tile_w(f(inp["w_out"][l]), 512) for l in range(NL)])
    w1 = np.stack([tile_w(f(inp["w_mlp_in"][l]), 512) for l in range(NL)])
    w2 = np.stack([np.ascontiguousarray(
        f(inp["w_mlp_out"][l]).reshape(2, 32, 128, 16, 128).transpose(0, 3, 2, 1, 4)) for l in range(NL)])
    cols = np.zeros((128, NL * NCOL + 16), np.float32)
    for l in range(NL):
        b = l * NCOL
        cols[:, b + 0:b + 16] = colize(f(inp["ln1_g"][l]))
        cols[:, b + 16:b + 32] = colize(f(inp["ln2_g"][l]))
        cols[:, b + 32:b + 40] = colize(f(inp["attn_norm_g"][l]))
        cols[:, b + 40:b + 48] = colize(f(inp["ssd_norm_g"][l]))
        cw = f(inp["conv_w"][l])
        for ch in range(12):
            for k in range(4):
                cols[:, b + 40 + 8 + ch * 4 + k] = cw[k, ch * 128:(ch + 1) * 128]
        cols[:, b + 96:b + 108] = colize(f(inp["conv_b"][l]))
        cols[:, b + 108:b + 116] = colize(np.repeat(f(inp["d_skip"][l]), 64))
    cols[:, NL * NCOL:] = colize(f(inp["final_norm_g"]))
    rows = np.stack([np.stack([f(inp["dt_bias"][l]), f(inp["a_log"][l])]) for l in range(NL)])
    return dict(win=win, wdt=wdt, wout=wout, w1=w1, w2=w2, cols=cols, rows=rows)


def kernel(**inputs):
    x = np.asarray(inputs["x"], dtype=np.float32)
    shared = prep_shared(inputs)
    nb = x.shape[0]
    in_maps = []
    for b in range(nb):
        m = dict(shared)
        m["xT"] = np.ascontiguousarray(x[b].T)
        in_maps.append(m)
    nc = Builder().build()
    res = run_bass_kernel_spmd(nc, in_maps, core_ids=list(range(nb)))
    out = np.stack([np.ascontiguousarray(res.results[b]["outT"].T) for b in range(nb)])
    return out.astype(np.float32)
```

```python
import contextlib
import numpy as np
import concourse.bass as bass
import concourse.mybir as mybir
from concourse.bass_utils import run_bass_kernel_spmd

F32 = mybir.dt.float32
BF16 = mybir.dt.bfloat16
I32 = mybir.dt.int32
AF = mybir.ActivationFunctionType
ALU = mybir.AluOpType

D = 2048
S_LEN = 4096
NL = 2
KC = 16
TT = 1024
NT = S_LEN // TT
EPS = 1e-5
NCOL = 116
DILS = (1, 4, 16)

COMPUTE = ('pe', 'act', 'dve', 'pool')
QUEUES = ('sp', 'act', 'pool')


class T:
    __slots__ = ('name', 'writers', 'readers', 'gen_deps', 'excl')

    def __init__(self, name='', excl=False):
        self.name = name
        self.writers = []
        self.readers = []
        self.gen_deps = []
        self.excl = excl


def TL(name, n):
    return [T('%s%d' % (name, i)) for i in range(n)]


def TX(name):
    return T(name, excl=True)


def TLX(name, n):
    return [T('%s%d' % (name, i), excl=True) for i in range(n)]


class Ins:
    __slots__ = ('eng', 'fn', 'deps', 'signal', 'sigval', 'is_dma', 'dsem', 'dval', 'prev_dma')

    def __init__(self, eng, fn, is_dma=False):
        self.eng = eng
        self.fn = fn
        self.deps = []
        self.signal = False
        self.sigval = None
        self.is_dma = is_dma
        self.dsem = None
        self.dval = None
        self.prev_dma = None


NPOOL = {'sp': 24, 'act': 8, 'pool': 12}


class SemState:
    def __init__(self, nc, st):
        self.csem = {e: st.enter_context(nc.semaphore('c_' + e)) for e in COMPUTE}
        self.dsem = {}
        for q in QUEUES:
            for j in range(NPOOL[q]):
                self.dsem[(q, j)] = st.enter_context(nc.semaphore('d_%s%d' % (q, j)))
        self.ccount = {e: 0 for e in COMPUTE}
        self.ndma = {q: 0 for q in QUEUES}


class Sched:
    def __init__(self, nc, sems, same_engine_sync=True):
        self.nc = nc
        self.sems = sems
        self.same = same_engine_sync
        self.lists = {e: [] for e in ('pe', 'act', 'dve', 'pool', 'sp')}
        self.ndma = dict(sems.ndma)
        self.ndma0 = dict(sems.ndma)
        self.npool = NPOOL
        self.dma_hist = {q: [] for q in QUEUES}

    def _deps_for(self, ins, reads, writes, partial):
        deps = []
        ex = []
        for t in list(reads) + list(writes):
            if t.excl and t not in ex:
                ex.append(t)
        reads = [t for t in reads if not t.excl]
        writes = [t for t in writes if not t.excl]
        for t in ex:
            deps.extend(t.writers)
            t.writers = [ins]
        for t in reads:
            deps.extend(t.writers)
        for t in writes:
            if partial:
                if t.readers or not t.writers:
                    t.gen_deps = list(t.readers) + list(t.writers)
                    t.writers = []
                    t.readers = []
                deps.extend(t.gen_deps)
            else:
                deps.extend(t.readers)
                deps.extend(t.writers)
        for t in reads:
            t.readers.append(ins)
        for t in writes:
            if partial:
                t.writers.append(ins)
            else:
                t.writers = [ins]
                t.readers = []
                t.gen_deps = []
        seen = set()
        out = []
        for d in deps:
            if id(d) in seen or d is ins:
                continue
            seen.add(id(d))
            out.append(d)
        ins.deps = out
        for d in out:
            if d.is_dma:
                continue
            if d.eng == ins.eng and not ins.is_dma and (ins.eng == 'pe' or not self.same):
                continue
            d.signal = True

    def op(self, eng, fn, reads=(), writes=(), partial=False):
        ins = Ins(eng, fn)
        self._deps_for(ins, reads, writes, partial)
        self.lists[eng].append(ins)
        return ins

    def dma(self, q, out, in_, reads=(), writes=(), partial=False, **kw):
        def fn(e, out=out, in_=in_, kw=kw):
            return e.dma_start(out=out, in_=in_, **kw)
        ins = Ins(q, fn, is_dma=True)
        self._deps_for(ins, reads, writes, partial)
        k = self.ndma[q]
        P = self.npool[q]
        ins.dsem = (q, k % P)
        ins.dval = 16 * (k // P + 1)
        hist = self.dma_hist[q]
        if k - self.ndma0[q] >= P:
            ins.prev_dma = hist[k - self.ndma0[q] - P]
        hist.append(ins)
        self.ndma[q] = k + 1
        self.lists[q].append(ins)
        return ins

    def emit(self):
        nc = self.nc
        with contextlib.ExitStack() as st:
            csem = self.sems.csem
            dsem = self.sems.dsem
            for e in COMPUTE:
                c = self.sems.ccount[e]
                for ins in self.lists[e]:
                    if ins.is_dma:
                        continue
                    if ins.signal:
                        c += 1
                        ins.sigval = c
                self.sems.ccount[e] = c
            for q in QUEUES:
                self.sems.ndma[q] = self.ndma[q]
            block = st.enter_context(nc.Block())
            same = self.same

            def run(engname, engobj):
                known = {}

                def wait(key, semh, val):
                    if known.get(key, 0) >= val:
                        return
                    known[key] = val
                    engobj.wait_ge(semh, val)

                for ins in self.lists[engname]:
                    for d in ins.deps:
                        if d.is_dma:
                            wait(d.dsem, dsem[d.dsem], d.dval)
                        else:
                            if d.eng == engname and not ins.is_dma:
                                if engname == 'pe' or not same:
                                    continue
                            wait(d.eng, csem[d.eng], d.sigval)
                    if ins.is_dma:
                        if ins.prev_dma is not None:
                            p = ins.prev_dma
                            wait(p.dsem, dsem[p.dsem], p.dval)
                        bi = ins.fn(engobj)
                        bi.then_inc(dsem[ins.dsem], 16)
                    else:
                        bi = ins.fn(engobj)
                        if ins.signal:
                            bi.then_inc(csem[engname], 1)
                if engname in QUEUES:
                    hist = self.dma_hist[engname]
                    P = self.npool[engname]
                    for ins in hist[-P:]:
                        wait(ins.dsem, dsem[ins.dsem], ins.dval)

            @block.tensor
            def _(e):
                run('pe', e)

            @block.scalar
            def _(e):
                run('act', e)

            @block.vector
            def _(e):
                run('dve', e)

            @block.gpsimd
            def _(e):
                run('pool', e)

            @block.sync
            def _(e):
                run('sp', e)


class Ops:
    def __init__(self, S):
        self.S = S

    def mm(self, out, lhsT, rhs, start, stop, reads, writes):
        return self.S.op('pe', lambda e: e.matmul(out, lhsT=lhsT, rhs=rhs, start=start, stop=stop),
                         reads=reads, writes=writes, partial=True)

    def tr(self, out, in_, ident, reads, writes):
        return self.S.op('pe', lambda e: e.transpose(out, in_, ident), reads=reads, writes=writes, partial=True)

    def act(self, out, in_, func, reads, writes, bias=None, scale=None, partial=False, eng='act'):
        kw = {}
        if bias is not None:
            kw['bias'] = bias
        if scale is not None:
            kw['scale'] = scale
        return self.S.op('act', lambda e: e.activation(out=out, in_=in_, func=func, **kw),
                         reads=reads, writes=writes, partial=partial)

    def tt(self, eng, out, in0, in1, op, reads, writes, partial=False):
        return self.S.op(eng, lambda e: e.tensor_tensor(out=out, in0=in0, in1=in1, op=op),
                         reads=reads, writes=writes, partial=partial)

    def ts(self, eng, out, in0, s1, op0, reads, writes, s2=None, op1=None, partial=False):
        if op1 is None:
            return self.S.op(eng, lambda e: e.tensor_scalar(out=out, in0=in0, scalar1=s1, scalar2=None, op0=op0),
                             reads=reads, writes=writes, partial=partial)
        return self.S.op(eng, lambda e: e.tensor_scalar(out=out, in0=in0, scalar1=s1, scalar2=s2, op0=op0, op1=op1),
                         reads=reads, writes=writes, partial=partial)

    def stt(self, eng, out, in0, scalar, in1, op0, op1, reads, writes, partial=False):
        return self.S.op(eng, lambda e: e.scalar_tensor_tensor(out=out, in0=in0, scalar=scalar, in1=in1, op0=op0, op1=op1),
                         reads=reads, writes=writes, partial=partial)

    def cp(self, eng, out, in_, reads, writes, partial=False):
        if eng == 'act':
            return self.S.op('act', lambda e: e.activation(out=out, in_=in_, func=AF.Copy),
                             reads=reads, writes=writes, partial=partial)
        return self.S.op(eng, lambda e: e.tensor_copy(out=out, in_=in_), reads=reads, writes=writes, partial=partial)

    def recip(self, out, in_, reads, writes, partial=False):
        return self.S.op('dve', lambda e: e.reciprocal(out=out, in_=in_), reads=reads, writes=writes, partial=partial)

    def memset(self, eng, ap, val, writes, partial=False):
        return self.S.op(eng, lambda e: e.memset(ap, val), writes=writes, partial=partial)


class Builder:
    def __init__(self, debug=(), nlayers=NL, phases=None):
        self.debug = set(debug)
        self.nlayers = nlayers
        self.phases = phases
        nc = bass.Bass("TRN2", target_bir_lowering=False)
        self.nc = nc
        dt = nc.dram_tensor

        def nlw(ph):
            if phases is None:
                return NL
            ls = [int(p[1]) for p in phases if p[0] == ph]
            return (max(ls) + 1) if ls else 0
        self.wshapes = {
            "win": [max(nlw('A'), 1), 11 if nlw('A') else 1, 128, KC, 512],
            "wdt": [max(nlw('A'), 1), 128, KC, 16],
            "wout": [max(nlw('D'), 1), 4 if nlw('D') else 1, 128, KC, 512],
            "w1": [max(nlw('E'), 1), 16 if nlw('E') else 1, 128, KC, 512],
            "w2": [max(nlw('E'), 1), 2, 16 if nlw('E') else 1, 128, 32, 128],
        }
        self.xT = dt("xT", [D, S_LEN], F32, kind="ExternalInput").ap()
        self.win = dt("win", self.wshapes["win"], F32, kind="ExternalInput").ap()
        self.wdt = dt("wdt", self.wshapes["wdt"], F32, kind="ExternalInput").ap()
        self.wout = dt("wout", self.wshapes["wout"], F32, kind="ExternalInput").ap()
        self.w1 = dt("w1", self.wshapes["w1"], F32, kind="ExternalInput").ap()
        self.w2 = dt("w2", self.wshapes["w2"], F32, kind="ExternalInput").ap()
        self.cols = dt("cols", [128, NL * NCOL + 16], F32, kind="ExternalInput").ap()
        self.rows = dt("rows", [NL, 2, 16], F32, kind="ExternalInput").ap()
        self.outT = dt("outT", [D, S_LEN], F32, kind="ExternalOutput").ap()

        def scratch(name, shape, dtype):
            kind = "ExternalOutput" if name in self.debug else "Internal"
            return dt(name, shape, dtype, kind=kind).ap()
        self.qT = [scratch("qT%d" % l, [1024, S_LEN], BF16) for l in range(NL)]
        self.kT = [scratch("kT%d" % l, [1024, S_LEN], BF16) for l in range(NL)]
        self.vT = [scratch("vT%d" % l, [1024, S_LEN], BF16) for l in range(NL)]
        self.zT = [scratch("zT%d" % l, [1024, S_LEN], F32) for l in range(NL)]
        self.xbcT = [scratch("xbcT%d" % l, [1536, S_LEN], F32) for l in range(NL)]
        self.dtr = [scratch("dtr%d" % l, [32, 128, 16], F32) for l in range(NL)]
        self.attn_raw = [scratch("attn_raw%d" % l, [1024, S_LEN], F32) for l in range(NL)]
        self.y_raw = [scratch("y_raw%d" % l, [1024, S_LEN], F32) for l in range(NL)]
        self.xa = [scratch("xa%d" % l, [D, S_LEN], F32) for l in range(NL)]
        self.xb = [scratch("xb%d" % l, [D, S_LEN], F32) for l in range(NL)]

    def phase(self, name, fn, *args):
        if self.phases is not None and name not in self.phases:
            return
        nc = self.nc
        with contextlib.ExitStack() as st:
            self.st = st
            self.pname = name
            self.S = Sched(nc, self.sems)
            self.O = Ops(self.S)
            fn(*args)
            self.S.emit()
        nc.all_engine_barrier()

    def sb(self, name, shape, dtype):
        return self.st.enter_context(self.nc.sbuf_tensor(self.pname + "_" + name, shape, dtype))

    def ps(self, name, shape, dtype=F32):
        return self.st.enter_context(self.nc.psum_tensor(self.pname + "_" + name, shape, dtype))

    def bank(self, name, dtype=F32):
        n = 512 if dtype == F32 else 1024
        return self.st.enter_context(self.nc.psum_tensor(self.pname + "_" + name, [128, n], dtype))

    def col(self, l, off, n=1):
        base = l * NCOL + off
        return self.colt[:, base:base + n]

    def load_consts(self, need_ident=False, need_tri=False):
        S, O = self.S, self.O
        self.colt = self.sb("colt", [128, NL * NCOL + 16], F32)
        self.t_col = T('colt')
        S.dma('sp', self.colt[:], self.cols, writes=[self.t_col])
        self.ones_bf = self.sb("ones_bf", [128, 128], BF16)
        self.t_ones = T('ones')
        O.memset('pool', self.ones_bf[:], 1.0, [self.t_ones])
        self.eps_c = self.sb("eps_c", [128, 1], F32)
        self.t_eps = T('eps')
        O.memset('pool', self.eps_c[:], EPS, [self.t_eps])
        if need_ident or need_tri:
            self.io_i = self.sb("io_i", [128, 256], I32)
            self.io_f = self.sb("io_f", [128, 256], F32)
            t_ii = T('io_i')
            self.t_iof = T('io_f')
            io_i = self.io_i
            S.op('pool', lambda e: e.iota(io_i[:].rearrange("p (k c) -> p k c", k=2), pattern=[[-128, 2], [1, 128]],
                                          base=128, channel_multiplier=-1), writes=[t_ii])
            O.cp('dve', self.io_f[:], self.io_i[:], [t_ii], [self.t_iof])
            self.ident = self.sb("ident", [128, 128], BF16)
            self.t_ident = T('ident')
            O.ts('dve', self.ident[:], self.io_f[:, 128:256], 0.0, ALU.is_equal, [self.t_iof], [self.t_ident])
        if need_tri:
            self.tri = self.sb("tri", [128, 128], F32)
            self.t_tri = T('tri')
            O.ts('dve', self.tri[:], self.io_f[:, 128:256], 0.0, ALU.is_ge, [self.t_iof], [self.t_tri])
            self.ones_f = self.sb("ones_f", [128, 128], F32)
            self.t_onesf = T('onesf')
            O.memset('pool', self.ones_f[:], 1.0, [self.t_onesf])
            self.one_c = self.sb("one_c", [128, 1], F32)
            self.t_onec = T('onec')
            O.memset('pool', self.one_c[:], 1.0, [self.t_onec])

    def rms_tile(self, src, nch, c0, ncols, gcol0_l, gcol_off, dst, dst_c0, t_dst, stage, t_stage, sq, t_sq, ps_stat, t_ps,
                 rstd, t_rstd, tag, out_f32_dram=None):
        S, O = self.S, self.O
        nfeat = nch * 128
        S.dma('sp', stage[:, 0:nch, :], src[:, c0:c0 + ncols].rearrange("(kc p) t -> p kc t", p=128), writes=[t_stage])
        for ch in range(nch):
            sqb = sq[ch % 2]
            O.act(sqb[:], stage[:, ch, :], AF.Square, [t_stage], [t_sq[ch % 2]])
            O.mm(ps_stat[:, 0:512], self.ones_bf[:], sqb[:], ch == 0, ch == nch - 1, [self.t_ones, t_sq[ch % 2]], [t_ps])
        O.act(rstd[:], ps_stat[:, 0:512], AF.Sqrt, [t_ps, self.t_eps], [t_rstd], bias=self.eps_c[:, 0:1], scale=1.0 / nfeat)
        O.recip(rstd[:], rstd[:], [t_rstd], [t_rstd])
        for ch in range(nch):
            eng = 'dve'
            gc = self.col(gcol0_l, gcol_off + ch)
            if out_f32_dram is None:
                O.stt(eng, dst[:, ch, dst_c0:dst_c0 + 512], stage[:, ch, :], gc, rstd[:], ALU.mult, ALU.mult,
                      [t_stage, t_rstd, self.t_col], [t_dst], partial=True)
            else:
                O.stt(eng, dst[:, ch, :], stage[:, ch, :], gc, rstd[:], ALU.mult, ALU.mult,
                      [t_stage, t_rstd, self.t_col], [t_dst], partial=True)
        if out_f32_dram is not None:
            S.dma('sp', out_f32_dram[:, c0:c0 + ncols].rearrange("(kc p) t -> p kc t", p=128), dst[:, 0:nch, :], reads=[t_dst])

    def phase_A(self, l, xin):
        S, O = self.S, self.O
        self.load_consts()
        hT = [self.sb("hT%d" % i, [128, KC, TT], BF16) for i in range(2)]
        t_hT = [[T('hT%d_%d' % (i, j)) for j in range(2)] for i in range(2)]
        xst = self.sb("xst", [128, KC, 512], F32)
        t_xst = T('xst')
        sq = [self.sb("sq%d" % i, [128, 512], BF16) for i in range(2)]
        t_sq = TL('sq', 2)
        rstd = self.sb("rstd", [128, 512], F32)
        t_rstd = T('rstd')
        NW = 3
        wb = [self.sb("wb%d" % i, [128, KC, 512], BF16) for i in range(NW)]
        t_wb = TL('wb', NW)
        wdtb = self.sb("wdtb", [128, KC, 16], BF16)
        t_wdt = T('wdt')
        NE = 4
        ev_bf = [self.sb("evb%d" % i, [128, TT], BF16) for i in range(NE)]
        ev_f = [self.sb("evf%d" % i, [128, TT], F32) for i in range(NE)]
        t_evb = TL('evb', NE)
        t_evf = TL('evf', NE)
        dts = self.sb("dts", [128, 8, 16], F32)
        t_dts = T('dts')
        ps_stat = self.bank("ps_stat")
        t_pss = TX('pss')
        NP = 4
        pmm = [self.bank("pmm%d" % i) for i in range(NP)]
        t_pmm = TLX('pmm', NP)
        ps_dt_b = self.bank("ps_dt")
        ps_dt = ps_dt_b[:, 0:128].rearrange("p (a b) -> p a b", a=8)
        t_psdt = TX('psdt')

        S.dma('pool', wdtb[:], self.wdt[l], writes=[t_wdt])

        def ln_tile(tt):
            hb = tt % 2
            for sub in range(2):
                self.rms_tile(xin, KC, tt * TT + sub * 512, 512, l, 0, hT[hb], sub * 512, t_hT[hb][sub], xst, t_xst,
                              sq, t_sq, ps_stat, t_pss, rstd, t_rstd, 'A')

        nw = 0
        wsched = [(tt, mb) for tt in range(NT) for mb in range(11)]

        def issue_w(i):
            if i < len(wsched):
                tt_, mb_ = wsched[i]
                S.dma('pool', wb[i % NW][:], self.win[l, mb_], writes=[t_wb[i % NW]])
        issue_w(0)
        issue_w(1)
        ln_tile(0)
        kps = 0
        kev = 0
        for tt in range(NT):
            hb = tt % 2
            tok0 = tt * TT
            for mb in range(11):
                wi = tt * 11 + mb
                issue_w(wi + 2)
                w = wb[wi % NW]
                tw = t_wb[wi % NW]
                if mb == 5 and tt + 1 < NT:
                    ln_tile(tt + 1)
                for s4 in range(4):
                    m = mb * 4 + s4
                    is_bf = m < 24
                    ei = kev % NE
                    kev += 1
                    ev = ev_bf[ei] if is_bf else ev_f[ei]
                    tev = t_evb[ei] if is_bf else t_evf[ei]
                    for th in range(2):
                        pi = kps % NP
                        kps += 1
                        for kc in range(KC):
                            O.mm(pmm[pi][:, 0:512], w[:, kc, s4 * 128:(s4 + 1) * 128], hT[hb][:, kc, th * 512:(th + 1) * 512],
                                 kc == 0, kc == KC - 1, [tw, t_hT[hb][th]], [t_pmm[pi]])
                        eng = 'act' if (kps % 2 == 0) else 'dve'
                        O.cp(eng, ev[:, th * 512:(th + 1) * 512], pmm[pi][:, 0:512], [t_pmm[pi]], [tev], partial=True)
                    if m < 8:
                        dst = self.qT[l][m * 128:(m + 1) * 128, tok0:tok0 + TT]
                    elif m < 16:
                        dst = self.kT[l][(m - 8) * 128:(m - 7) * 128, tok0:tok0 + TT]
                    elif m < 24:
                        dst = self.vT[l][(m - 16) * 128:(m - 15) * 128, tok0:tok0 + TT]
                    elif m < 32:
                        dst = self.zT[l][(m - 24) * 128:(m - 23) * 128, tok0:tok0 + TT]
                    else:
                        dst = self.xbcT[l][(m - 32) * 128:(m - 31) * 128, tok0:tok0 + TT]
                    S.dma('sp', dst, ev[:], reads=[tev])
            for tb in range(8):
                for kc in range(KC):
                    O.mm(ps_dt[:, tb, :], hT[hb][:, kc, tb * 128:(tb + 1) * 128], wdtb[:, kc, :], kc == 0, kc == KC - 1,
                         [t_wdt, t_hT[hb][tb // 4]], [t_psdt])
            O.cp('dve', dts[:], ps_dt, [t_psdt], [t_dts])
            S.dma('sp', self.dtr[l][tt * 8:(tt + 1) * 8].rearrange("c p h -> p c h"), dts[:], reads=[t_dts])

    def phase_B(self, l):
        S, O = self.S, self.O
        self.load_consts(need_ident=True)
        qc = [self.sb("qc%d" % i, [128, S_LEN], BF16) for i in range(2)]
        kc_ = [self.sb("kc%d" % i, [128, S_LEN], BF16) for i in range(2)]
        vc = [self.sb("vc%d" % i, [128, S_LEN], BF16) for i in range(2)]
        t_q = TL('q', 2)
        t_k = TL('k', 2)
        t_v = TL('v', 2)
        acc = self.sb("acc", [128, 2, S_LEN], F32)
        t_acc = TL('acc', 8)
        Vt = [self.sb("Vt%d" % i, [128, 32, 2, 128], BF16) for i in range(2)]
        t_Vt = [TL('Vt%d_' % i, 32) for i in range(2)]
        Et = [self.sb("Et%d" % i, [128, 2, 256], F32) for i in range(3)]
        t_Et = TL('Et', 3)
        Dc = self.sb("Dc", [128, 256], F32)
        msk = self.sb("msk", [128, 256], F32)
        t_Dc = T('Dc')
        t_msk = T('msk')
        NPB = 3
        pexp = [self.sb("pexp%d" % i, [128, 2, 256], F32) for i in range(NPB)]
        Pb = [self.sb("Pb%d" % i, [128, 2, 256], BF16) for i in range(NPB)]
        t_pexp = TL('pexp', NPB)
        t_Pb = TL('Pb', NPB)
        rec = self.sb("rec", [128, S_LEN], F32)
        t_rec = T('rec')
        ob = self.sb("ob", [128, S_LEN], F32)
        t_ob = T('ob')
        ps_s = [self.ps("ps_s%d" % i, [128, 2, 512]) for i in range(2)]
        t_pss = [TLX('pss%d_' % i, 2) for i in range(2)]
        ps_o_b = [self.bank("ps_o%d" % i) for i in range(2)]
        ps_o = [x[:, 0:256].rearrange("p (a b) -> p a b", a=2) for x in ps_o_b]
        t_pso = TLX('pso', 2)
        ps_t_b = [self.bank("ps_t%d" % i, BF16) for i in range(2)]
        ps_t = [x[:, 0:512].rearrange("p (a b) -> p a b", a=4) for x in ps_t_b]
        t_pst = TLX('pst', 2)

        O.ts('dve', Dc[:], self.io_f[:], 0.0, ALU.max, [self.t_iof], [t_Dc], s2=128.0, op1=ALU.min)
        O.ts('dve', msk[:], self.io_f[:], 0.0, ALU.is_ge, [self.t_iof], [t_msk])
        O.stt('dve', msk[:], self.io_f[:], 128.0, msk[:], ALU.is_le, ALU.mult, [self.t_iof, t_msk], [t_msk])
        import os
        KB = os.environ.get("KB", "")
        for i in range(2):
            if 'noms' in KB:
                break
            O.memset('pool', Vt[i][:, :, 0, 64:128], 1.0, t_Vt[i], partial=True)
            O.memset('pool', Vt[i][:, :, 1, 0:64], 1.0, t_Vt[i], partial=True)

        def load_pair(hp):
            b = hp % 2
            r0 = hp * 128
            S.dma('sp', qc[b][:], self.qT[l][r0:r0 + 128, :], writes=[t_q[b]])
            S.dma('sp', kc_[b][:], self.kT[l][r0:r0 + 128, :], writes=[t_k[b]])
            S.dma('sp', vc[b][:], self.vT[l][r0:r0 + 128, :], writes=[t_v[b]])
        load_pair(0)
        ku = 0
        kt = 0
        kvt = 0
        import os
        KB = os.environ.get("KB", "")
        nhp = int(os.environ.get("KB_HP", "8"))
        nbr = int(os.environ.get("KB_BR", "3"))
        for hp in range(nhp):
            pb_ = hp % 2
            if hp + 1 < 8:
                load_pair(hp + 1)
            Q, K, V = qc[pb_], kc_[pb_], vc[pb_]
            for b in range(3):
                if 'noE' in KB:
                    break
                for hh in range(2):
                    h = 2 * hp + hh
                    slope = 2.0 ** (-8.0 * (h + 1) / 16.0)
                    O.act(Et[b][:, hh, :], Dc[:], AF.Exp, [t_Dc], [t_Et[b]], scale=-slope * DILS[b], partial=True)
                O.tt('dve', Et[b][:], Et[b][:], msk[:].unsqueeze(1).to_broadcast([128, 2, 256]), ALU.mult,
                     [t_Et[b], t_msk], [t_Et[b]])
            for b in range(nbr):
                dil = DILS[b]
                nblk = 32 // dil
                vb = kvt % 2
                kvt += 1
                VT = Vt[vb]
                tVT = t_Vt[vb]

                def tslice(r, n):
                    base = r + dil * 128 * n
                    return slice(base, base + dil * 127 + 1, dil)
                for kb0 in range(0, 32, 4):
                    if 'noV' in KB:
                        break
                    if 'V1' in KB and kb0 >= 4:
                        break
                    if 'V2' in KB and kb0 >= 8:
                        break
                    if 'V3' in KB and kb0 >= 12:
                        break
                    pt = ps_t[kt % 2]
                    tpt = t_pst[kt % 2]
                    kt += 1
                    for i4 in range(4):
                        kb = kb0 + i4
                        r, n = kb // nblk, kb % nblk
                        O.tr(pt[:, i4, :], V[:, tslice(r, n)], self.ident[:], [t_v[pb_], self.t_ident], [tpt])
                    if 'nocp' in KB:
                        continue
                    O.cp('act', VT[:, kb0:kb0 + 4, 0, 0:64], pt[:, :, 0:64], [tpt], tVT[kb0:kb0 + 4], partial=True)
                    O.cp('dve', VT[:, kb0:kb0 + 4, 1, 64:128], pt[:, :, 64:128], [tpt], tVT[kb0:kb0 + 4], partial=True)
                ust = {}

                def emit_qk(kb):
                    nonlocal ku
                    r, n = kb // nblk, kb % nblk
                    qs = tslice(r, n)
                    si = ku % 2
                    pi = ku % NPB
                    ku += 1
                    pss, tps = ps_s[si], t_pss[si]
                    lo = 0 if n > 0 else 128
                    for hh in range(2):
                        pr = slice(hh * 64, hh * 64 + 64)
                        if n > 0:
                            O.mm(pss[:, hh, 0:128], K[pr, tslice(r, n - 1)], Q[pr, qs], True, True, [t_k[pb_], t_q[pb_]], [tps[hh]])
                        O.mm(pss[:, hh, 128:256], K[pr, qs], Q[pr, qs], True, True, [t_k[pb_], t_q[pb_]], [tps[hh]])
                    O.act(pexp[pi][:, :, lo:256], pss[:, :, lo:256], AF.Exp, tps, [t_pexp[pi]], scale=0.125)
                    meng = 'dve' if kb % 3 == 2 else 'pool'
                    O.tt(meng, Pb[pi][:, :, lo:256], pexp[pi][:, :, lo:256], Et[b][:, :, lo:256], ALU.mult,
                         [t_pexp[pi], t_Et[b]], [t_Pb[pi]])
                    ust[kb] = (r, n, qs, si, pi)

                def emit_pv(kb):
                    r, n, qs, si, pi = ust.pop(kb)
                    pso, tpo = ps_o[si], t_pso[si]
                    for hh in range(2):
                        if n > 0:
                            O.mm(pso[:, hh, :], VT[:, kb - 1, hh, :], Pb[pi][:, hh, 0:128], True, False, [tVT[kb - 1], t_Pb[pi]], [tpo])
                        O.mm(pso[:, hh, :], VT[:, kb, hh, :], Pb[pi][:, hh, 128:256], n == 0, True, [tVT[kb], t_Pb[pi]], [tpo])
                    base = r + dil * 128 * n
                    blks = list(range(base // 512, (base + dil * 127) // 512 + 1))
                    tacc = [t_acc[x] for x in blks]
                    if b == 0:
                        O.cp('act', acc[:, :, qs], pso, [tpo], tacc, partial=True)
                    else:
                        O.tt('dve', acc[:, :, qs], acc[:, :, qs], pso, ALU.add, [tpo] + tacc, tacc)

                emit_qk(0)
                for kb in range(32):
                    if kb + 1 < 32:
                        emit_qk(kb + 1)
                    emit_pv(kb)
            if 'nofin' in KB:
                continue
            O.recip(rec[0:64, :], acc[64:128, 0, :], t_acc, [t_rec], partial=True)
            O.recip(rec[64:128, :], acc[0:64, 1, :], t_acc, [t_rec], partial=True)
            O.tt('pool', ob[0:64, :], acc[0:64, 0, :], rec[0:64, :], ALU.mult, t_acc + [t_rec], [t_ob], partial=True)
            O.tt('pool', ob[64:128, :], acc[64:128, 1, :], rec[64:128, :], ALU.mult, t_acc + [t_rec], [t_ob], partial=True)
            S.dma('sp', self.attn_raw[l][hp * 128:(hp + 1) * 128, :], ob[:], reads=[t_ob])

    def phase_C(self, l):
        S, O = self.S, self.O
        self.load_consts(need_ident=True, need_tri=True)
        tri, ones_f = self.tri, self.ones_f
        dtb = self.sb("dtb", [128, 16], F32)
        aneg = self.sb("aneg", [128, 16], F32)
        t_dtb, t_an = T('dtb'), T('aneg')
        S.dma('sp', dtb[:], self.rows[l, 0].partition_broadcast(128), writes=[t_dtb])
        S.dma('sp', aneg[:], self.rows[l, 1].partition_broadcast(128), writes=[t_an])
        O.act(aneg[:], aneg[:], AF.Exp, [t_an], [t_an])
        O.ts('dve', aneg[:], aneg[:], -1.0, ALU.mult, [t_an], [t_an])
        dt_all = self.sb("dt_all", [128, 32, 16], F32)
        dA_all = self.sb("dA_all", [128, 32, 16], F32)
        t_dt, t_dA = T('dt'), T('dA')
        S.dma('sp', dt_all[:], self.dtr[l].rearrange("c p h -> p c h"), writes=[t_dt])
        O.tt('dve', dt_all[:], dt_all[:], dtb[:].unsqueeze(1).to_broadcast([128, 32, 16]), ALU.add, [t_dt, t_dtb], [t_dt])
        O.act(dt_all[:], dt_all[:], AF.Exp, [t_dt], [t_dt])
        O.act(dt_all[:], dt_all[:], AF.Ln, [t_dt, self.t_onec], [t_dt], bias=self.one_c[:, 0:1])
        O.tt('dve', dA_all[:], dt_all[:], aneg[:].unsqueeze(1).to_broadcast([128, 32, 16]), ALU.mult, [t_dt, t_an], [t_dA])

        xr = self.sb("xr", [128, 12, 515], F32)
        t_xr = T('xr')
        cv = self.sb("cv", [128, 12, 512], F32)
        t_cv = TL('cv', 12)
        xs_bf = self.sb("xs_bf", [128, 8, 512], BF16)
        t_xsb = T('xsb')
        bcf = self.sb("bcf", [128, 4, 512], BF16)
        t_bcf = T('bcf')
        zt = self.sb("zt", [128, 8, 512], F32)
        t_zt = T('zt')
        yst = self.sb("yst", [128, 8, 512], F32)
        t_yst = T('yst')
        def mk2(name, shape, dt_):
            return [self.sb("%s_%d" % (name, i), shape, dt_) for i in range(2)]
        cbm2, t_cbm2 = mk2("cbm", [128, 2, 128], BF16), TL('cbm', 2)
        Btok2, t_Btok2 = mk2("Btok", [128, 2, 128], BF16), TL('Btok', 2)
        Xdt2, t_Xdt2 = mk2("Xdt", [128, 16, 64], BF16), TL('Xdt', 2)
        Xw2, t_Xw2 = mk2("Xw", [128, 16, 64], BF16), TL('Xw', 2)
        nacs2, t_nacs2 = mk2("nacs", [128, 16], F32), TL('nacs', 2)
        dAtri2, t_dAtri2 = mk2("dAtri", [128, 16, 128], F32), [TL('dAtri%d_' % i, 4) for i in range(2)]
        E22, t_E22 = mk2("E2", [128, 16, 128], F32), [TL('E2%d_' % i, 4) for i in range(2)]
        Lraw2, t_Lraw2 = mk2("Lraw", [128, 16, 128], F32), [TL('Lraw%d_' % i, 16) for i in range(2)]
        MT2, t_MT2 = mk2("MT", [128, 16, 128], BF16), [TL('MT%d_' % i, 16) for i in range(2)]
        Cp2, t_Cp2 = mk2("Cp", [128, 16, 128], BF16), [TL('Cp%d_' % i, 4) for i in range(2)]
        tmp = self.sb("tmp", [128, 8, 128], F32)
        t_tmp = TL('tmp', 2)
        R = self.sb("R", [128, 16, 64], F32)
        t_R = T('R')
        Rbf = self.sb("Rbf", [128, 16, 64], BF16)
        t_Rbf = T('Rbf')

        ps_misc = self.bank("ps_misc")
        ps_cb = ps_misc[:, 0:256].rearrange("p (g l) -> p g l", g=2)
        ps_a = ps_misc[:, 256:272]
        t_pcb = TX('pmisc')
        t_pa = t_pcb
        ps_tx_b = self.bank("ps_tx", BF16)
        ps_tx = ps_tx_b[:, :].rearrange("p (j c) -> p j c", j=8)
        t_ptx = TX('ptx')
        ps_tb_b = self.bank("ps_tb", BF16)
        ps_tb = ps_tb_b[:, 0:256].rearrange("p (j c) -> p j c", j=2)
        t_ptb = TX('ptb')
        ps_bc_b = [self.bank("ps_bc%d" % i) for i in range(2)]
        ps_bc = [x[:, :].rearrange("p (h l) -> p h l", h=4) for x in ps_bc_b]
        t_pbc = TLX('pbc', 2)
        ps_y = self.ps("ps_y", [128, 8, 128])
        t_py = TLX('py', 2)
        ps_st = self.bank("ps_st")
        t_pst = TX('pst')

        O.memset('pool', xr[:, :, 0:3], 0.0, [t_xr])
        cw0 = 48
        kbc = 0
        for t8 in range(8):
            c0 = t8 * 512
            if t8 == 0:
                S.dma('sp', xr[:, :, 3:515], self.xbcT[l][:, 0:512].rearrange("(c p) t -> p c t", p=128), writes=[t_xr], partial=True)
            else:
                S.dma('sp', xr[:, :, :], self.xbcT[l][:, c0 - 3:c0 + 512].rearrange("(c p) t -> p c t", p=128), writes=[t_xr])
            S.dma('sp', zt[:], self.zT[l][:, c0:c0 + 512].rearrange("(c p) t -> p c t", p=128), writes=[t_zt])
            for ch in range(12):
                eng = 'dve'
                O.ts(eng, cv[:, ch, :], xr[:, ch, 0:512], self.col(l, cw0 + ch * 4 + 0), ALU.mult, [t_xr, self.t_col], [t_cv[ch]],
                     s2=self.col(l, 96 + ch), op1=ALU.add)
                for k in range(1, 4):
                    O.stt(eng, cv[:, ch, :], xr[:, ch, k:k + 512], self.col(l, cw0 + ch * 4 + k), cv[:, ch, :], ALU.mult, ALU.add,
                          [t_xr, t_cv[ch], self.t_col], [t_cv[ch]])
            O.act(cv[:, 0:8, :], cv[:, 0:8, :], AF.Silu, t_cv[0:8], t_cv[0:8])
            O.act(bcf[:], cv[:, 8:12, :], AF.Silu, t_cv[8:12], [t_bcf])
            O.cp('pool', xs_bf[:], cv[:, 0:8, :], t_cv[0:8], [t_xsb])
            O.act(zt[:], zt[:], AF.Silu, [t_zt], [t_zt])
            for j in range(8):
                eng = 'dve' if j % 2 == 0 else 'pool'
                O.ts(eng, cv[:, j, :], cv[:, j, :], self.col(l, 108 + j), ALU.mult, [t_cv[j], self.t_col], [t_cv[j]])
            def stage1(c, cc):
                nonlocal kbc
                bi = cc % 2
                cbm, t_cbm, Btok, t_Btok = cbm2[bi], t_cbm2[bi], Btok2[bi], t_Btok2[bi]
                Xdt, t_Xdt, Xw, t_Xw = Xdt2[bi], t_Xdt2[bi], Xw2[bi], t_Xw2[bi]
                nacs, t_nacs = nacs2[bi], t_nacs2[bi]
                dAtri, t_dAtri, E2, t_E2 = dAtri2[bi], t_dAtri2[bi], E22[bi], t_E22[bi]
                Lraw, t_Lraw, MT, t_MT, Cp, t_Cp = Lraw2[bi], t_Lraw2[bi], MT2[bi], t_MT2[bi], Cp2[bi], t_Cp2[bi]
                cs = slice(c * 128, (c + 1) * 128)
                for g in range(2):
                    O.mm(ps_cb[:, g, :], bcf[:, g, cs], bcf[:, 2 + g, cs], True, True, [t_bcf], [t_pcb])
                O.tt('dve', cbm[:], ps_cb, tri[:].unsqueeze(1).to_broadcast([128, 2, 128]), ALU.mult, [t_pcb, self.t_tri], [t_cbm])
                for g in range(2):
                    O.tr(ps_tb[:, g, :], bcf[:, g, cs], self.ident[:], [t_bcf, self.t_ident], [t_ptb])
                O.cp('act', Btok[:], ps_tb, [t_ptb], [t_Btok])
                for j in range(8):
                    O.tr(ps_tx[:, j, :], xs_bf[:, j, cs], self.ident[:], [t_xsb, self.t_ident], [t_ptx])
                O.tt('dve', Xdt[:], ps_tx_b[:, :].rearrange("p (h d) -> p h d", h=16),
                     dt_all[:, cc, :].unsqueeze(2).to_broadcast([128, 16, 64]), ALU.mult, [t_ptx, t_dt], [t_Xdt])
                O.mm(ps_a, tri[:], dA_all[:, cc, :], True, True, [self.t_tri, t_dA], [t_pa])
                O.ts('dve', nacs[:], ps_a, -1.0, ALU.mult, [t_pa], [t_nacs])
                for bq in range(4):
                    hs = slice(4 * bq, 4 * bq + 4)
                    g = bq // 2
                    O.tt('pool', dAtri[:, hs, :], tri[:].unsqueeze(1).to_broadcast([128, 4, 128]),
                         dA_all[:, cc, hs].unsqueeze(2).to_broadcast([128, 4, 128]), ALU.mult, [self.t_tri, t_dA], [t_dAtri[bq]])
                    pbc, tpbc, pbcb = ps_bc[kbc % 2], t_pbc[kbc % 2], ps_bc_b[kbc % 2]
                    kbc += 1
                    O.mm(pbcb[:, :], ones_f[:], dAtri[:, hs, :].rearrange("p h l -> p (h l)"), True, True,
                         [self.t_onesf, t_dAtri[bq]], [tpbc])
                    O.act(E2[:, hs, :], pbc, AF.Exp, [tpbc], [t_E2[bq]])
                    for hl in range(4):
                        h = 4 * bq + hl
                        O.ts('dve', Lraw[:, h, :], pbc[:, hl, :], nacs[:, h:h + 1], ALU.add, [tpbc, t_nacs], [t_Lraw[h]],
                             s2=0.0, op1=ALU.min)
                    O.act(Lraw[:, hs, :], Lraw[:, hs, :], AF.Exp, t_Lraw[4 * bq:4 * bq + 4], t_Lraw[4 * bq:4 * bq + 4])
                    O.tt('pool', MT[:, hs, :], Lraw[:, hs, :], cbm[:, g, :].unsqueeze(1).to_broadcast([128, 4, 128]), ALU.mult,
                         t_Lraw[4 * bq:4 * bq + 4] + [t_cbm], t_MT[4 * bq:4 * bq + 4])
                    O.tt('pool', Cp[:, hs, :], E2[:, hs, :], bcf[:, 2 + g, cs].unsqueeze(1).to_broadcast([128, 4, 128]), ALU.mult,
                         [t_E2[bq], t_bcf], [t_Cp[bq]])
                O.tt('pool', Xw[:], Xdt[:], Lraw[:, :, 127:128].to_broadcast([128, 16, 64]), ALU.mult, [t_Xdt] + t_Lraw, [t_Xw])

            def stage2(c, cc):
                bi = cc % 2
                Btok, t_Btok = Btok2[bi], t_Btok2[bi]
                Xdt, t_Xdt, Xw, t_Xw = Xdt2[bi], t_Xdt2[bi], Xw2[bi], t_Xw2[bi]
                E2, t_E2 = E22[bi], t_E22[bi]
                MT, t_MT, Cp, t_Cp = MT2[bi], t_MT2[bi], Cp2[bi], t_Cp2[bi]
                cs = slice(c * 128, (c + 1) * 128)
                for j in range(8):
                    for hh in range(2):
                        h = 2 * j + hh
                        o = ps_y[hh * 64:(hh + 1) * 64, j, :]
                        O.mm(o, Xdt[:, h, :], MT[:, h, :], True, cc == 0, [t_Xdt, t_MT[h]], [t_py[j // 4]])
                        if cc > 0:
                            O.mm(o, Rbf[:, h, :], Cp[:, h, :], False, True, [t_Rbf, t_Cp[h // 4]], [t_py[j // 4]])
                for yb in range(2):
                    js = slice(4 * yb, 4 * yb + 4)
                    O.tt('dve', tmp[:, js, :], ps_y[:, js, :], cv[:, js, cs], ALU.add, [t_py[yb]] + t_cv[4 * yb:4 * yb + 4], [t_tmp[yb]])
                    O.tt('pool', yst[:, js, cs], tmp[:, js, :], zt[:, js, cs], ALU.mult, [t_tmp[yb], t_zt], [t_yst], partial=True)
                if cc < 31:
                    for g in range(2):
                        gs = slice(8 * g, 8 * g + 8)
                        O.mm(ps_st[:, :], Btok[:, g, :], Xw[:, gs, :].rearrange("p h d -> p (h d)"), True, True,
                             [t_Btok, t_Xw], [t_pst])
                        stv = ps_st[:, :].rearrange("p (h d) -> p h d", h=8)
                        if cc == 0:
                            O.cp('dve', R[:, gs, :], stv, [t_pst], [t_R], partial=True)
                        else:
                            O.tt('dve', R[:, gs, :], R[:, gs, :], E2[:, gs, 127:128].to_broadcast([128, 8, 64]), ALU.mult,
                                 [t_R] + t_E2, [t_R])
                            O.tt('dve', R[:, gs, :], R[:, gs, :], stv, ALU.add, [t_R, t_pst], [t_R])
                    O.cp('act', Rbf[:], R[:], [t_R], [t_Rbf])

            stage1(0, t8 * 4)
            for c in range(4):
                if c + 1 < 4:
                    stage1(c + 1, t8 * 4 + c + 1)
                stage2(c, t8 * 4 + c)
            S.dma('sp', self.y_raw[l][:, c0:c0 + 512].rearrange("(c p) t -> p c t", p=128), yst[:], reads=[t_yst])

    def phase_D(self, l, xin):
        S, O = self.S, self.O
        self.load_consts()
        mixT = [self.sb("mixT%d" % i, [128, KC, TT], BF16) for i in range(2)]
        t_mix = [[T('mix%d_%d' % (i, j)) for j in range(2)] for i in range(2)]
        stg = self.sb("stg", [128, 8, 512], F32)
        t_stg = T('stg')
        sq = [self.sb("sq%d" % i, [128, 512], BF16) for i in range(2)]
        t_sq = TL('sq', 2)
        rstd = self.sb("rstd", [128, 512], F32)
        t_rstd = T('rstd')
        NW = 3
        wb = [self.sb("wb%d" % i, [128, KC, 512], BF16) for i in range(NW)]
        t_wb = TL('wb', NW)
        NE = 4
        xres = [self.sb("xres%d" % i, [128, TT], F32) for i in range(NE)]
        t_xres = TL('xres', NE)
        ps_stat = self.bank("ps_stat")
        t_pss = TX('pss')
        NP = 4
        pmm = [self.bank("pmm%d" % i) for i in range(NP)]
        t_pmm = TLX('pmm', NP)

        class View:
            def __init__(self, t, off):
                self.t, self.off = t, off

            def __getitem__(self, key):
                p, ch, c = key
                return self.t[p, ch + self.off, c]

        def norm_tile(tt):
            hb = tt % 2
            for sub in range(2):
                c0 = tt * TT + sub * 512
                self.rms_tile(self.attn_raw[l], 8, c0, 512, l, 32, View(mixT[hb], 0), sub * 512, t_mix[hb][sub], stg, t_stg,
                              sq, t_sq, ps_stat, t_pss, rstd, t_rstd, 'D')
                for g in range(2):
                    self.rms_tile(self.y_raw[l][g * 512:(g + 1) * 512, :], 4, c0, 512, l, 40 + 4 * g, View(mixT[hb], 8 + 4 * g),
                                  sub * 512, t_mix[hb][sub], stg, t_stg, sq, t_sq, ps_stat, t_pss, rstd, t_rstd, 'D')

        wsched = [(tt, mb) for tt in range(NT) for mb in range(4)]

        def issue_w(i):
            if i < len(wsched):
                tt_, mb_ = wsched[i]
                S.dma('pool', wb[i % NW][:], self.wout[l, mb_], writes=[t_wb[i % NW]])
        issue_w(0)
        issue_w(1)
        norm_tile(0)
        kps = 0
        kev = 0
        for tt in range(NT):
            hb = tt % 2
            tok0 = tt * TT
            for mb in range(4):
                wi = tt * 4 + mb
                issue_w(wi + 2)
                w, tw = wb[wi % NW], t_wb[wi % NW]
                if mb == 1 and tt + 1 < NT:
                    norm_tile(tt + 1)
                for s4 in range(4):
                    m = mb * 4 + s4
                    ei = kev % NE
                    kev += 1
                    xr_, txr = xres[ei], t_xres[ei]
                    S.dma('sp', xr_[:], xin[m * 128:(m + 1) * 128, tok0:tok0 + TT], writes=[txr])
                    for th in range(2):
                        pi = kps % NP
                        kps += 1
                        for kc in range(KC):
                            O.mm(pmm[pi][:, 0:512], w[:, kc, s4 * 128:(s4 + 1) * 128], mixT[hb][:, kc, th * 512:(th + 1) * 512],
                                 kc == 0, kc == KC - 1, [tw, t_mix[hb][th]], [t_pmm[pi]])
                        O.tt('dve', xr_[:, th * 512:(th + 1) * 512], xr_[:, th * 512:(th + 1) * 512], pmm[pi][:, 0:512], ALU.add,
                             [t_pmm[pi], txr], [txr])
                    S.dma('sp', self.xa[l][m * 128:(m + 1) * 128, tok0:tok0 + TT], xr_[:], reads=[txr])

    def phase_E(self, l):
        S, O = self.S, self.O
        self.load_consts()
        xa, xb = self.xa[l], self.xb[l]
        hT = self.sb("h2T", [128, KC, TT], BF16)
        t_hT = TL('h2T', 2)
        aT = self.sb("aT", [128, 32, TT], BF16)
        t_aT = [[T('aT%d_%d' % (i, j)) for j in range(2)] for i in range(32)]
        xst = self.sb("xst", [128, KC, 512], F32)
        t_xst = T('xst')
        sq = [self.sb("sq%d" % i, [128, 512], BF16) for i in range(2)]
        t_sq = TL('sq', 2)
        rstd = self.sb("rstd", [128, 512], F32)
        t_rstd = T('rstd')
        NW = 3
        wb = [self.sb("wb%d" % i, [128, 8192], BF16) for i in range(NW)]
        t_wb = TL('wb', NW)
        NR = 3
        rl = [self.sb("rl%d" % i, [128, 512], F32) for i in range(NR)]
        t_rl = TL('rl', NR)
        NE = 4
        xres = [self.sb("xres%d" % i, [128, TT], F32) for i in range(NE)]
        t_xres = TL('xres', NE)
        ps_stat = self.bank("ps_stat")
        t_pss = TX('pss')
        NP = 6
        pmm = [self.bank("pmm%d" % i) for i in range(NP)]
        t_pmm = TLX('pmm', NP)

        wsched = []
        for tt in range(NT):
            for half in range(2):
                for b in range(8):
                    wsched.append(('w1', half * 8 + b))
                for j in range(16):
                    wsched.append(('w2', half, j))

        def issue_w(i):
            if i < len(wsched):
                e = wsched[i]
                buf = wb[i % NW]
                if e[0] == 'w1':
                    S.dma('pool', buf[:].rearrange("p (k c) -> p k c", k=KC), self.w1[l, e[1]], writes=[t_wb[i % NW]])
                else:
                    S.dma('pool', buf[:, 0:4096].rearrange("p (k c) -> p k c", k=32), self.w2[l, e[1], e[2]], writes=[t_wb[i % NW]])
        issue_w(0)
        issue_w(1)
        wi = 0
        kps = 0
        kev = 0
        krl = 0
        for tt in range(NT):
            tok0 = tt * TT
            for sub in range(2):
                self.rms_tile(xa, KC, tok0 + sub * 512, 512, l, 16, hT, sub * 512, t_hT[sub], xst, t_xst, sq, t_sq, ps_stat, t_pss,
                              rstd, t_rstd, 'E')
            for half in range(2):
                for b in range(8):
                    issue_w(wi + 2)
                    w = wb[wi % NW][:].rearrange("p (k c) -> p k c", k=KC)
                    tw = t_wb[wi % NW]
                    wi += 1
                    for s4 in range(4):
                        fc = b * 4 + s4
                        for th in range(2):
                            pi = kps % NP
                            kps += 1
                            for kc in range(KC):
                                O.mm(pmm[pi][:, 0:512], w[:, kc, s4 * 128:(s4 + 1) * 128], hT[:, kc, th * 512:(th + 1) * 512],
                                     kc == 0, kc == KC - 1, [tw, t_hT[th]], [t_pmm[pi]])
                            ri = krl % NR
                            krl += 1
                            O.act(rl[ri][:], pmm[pi][:, 0:512], AF.Relu, [t_pmm[pi]], [t_rl[ri]])
                            eng = 'dve' if krl % 2 == 0 else 'pool'
                            O.tt(eng, aT[:, fc, th * 512:(th + 1) * 512], rl[ri][:], rl[ri][:], ALU.mult, [t_rl[ri]], [t_aT[fc][th]])
                for j in range(16):
                    issue_w(wi + 2)
                    w = wb[wi % NW][:, 0:4096].rearrange("p (k c) -> p k c", k=32)
                    tw = t_wb[wi % NW]
                    wi += 1
                    ei = kev % NE
                    kev += 1
                    xr_, txr = xres[ei], t_xres[ei]
                    src = xa if half == 0 else xb
                    S.dma('sp', xr_[:], src[j * 128:(j + 1) * 128, tok0:tok0 + TT], reads=[], writes=[txr])
                    for th in range(2):
                        pi = kps % NP
                        kps += 1
                        for kc in range(32):
                            O.mm(pmm[pi][:, 0:512], w[:, kc, :], aT[:, kc, th * 512:(th + 1) * 512], kc == 0, kc == 31,
                                 [tw, t_aT[kc][th]], [t_pmm[pi]])
                        O.tt('dve', xr_[:, th * 512:(th + 1) * 512], xr_[:, th * 512:(th + 1) * 512], pmm[pi][:, 0:512], ALU.add,
                             [t_pmm[pi], txr], [txr])
                    S.dma('sp', xb[j * 128:(j + 1) * 128, tok0:tok0 + TT], xr_[:], reads=[txr])

    def phase_F(self, xin):
        S, O = self.S, self.O
        self.load_consts()
        xst = [self.sb("xst%d" % i, [128, KC, 512], F32) for i in range(2)]
        t_xst = TL('xst', 2)
        sq = [self.sb("sq%d" % i, [128, 512], BF16) for i in range(2)]
        t_sq = TL('sq', 2)
        rstd = self.sb("rstd", [128, 512], F32)
        t_rstd = T('rstd')
        ps_stat = [self.bank("ps_stat%d" % i) for i in range(2)]
        t_pss = TLX('pss', 2)
        ost = [self.sb("ost%d" % i, [128, KC, 512], F32) for i in range(2)]
        t_ost = TL('ost', 2)
        for i in range(8):
            self.rms_tile(xin, KC, i * 512, 512, NL, 0, ost[i % 2], 0, t_ost[i % 2], xst[i % 2], t_xst[i % 2], sq, t_sq,
                          ps_stat[i % 2], t_pss[i % 2], rstd, t_rstd, 'F', out_f32_dram=self.outT)

    def build(self):
        x = self.xT
        with contextlib.ExitStack() as gst:
            self.sems = SemState(self.nc, gst)
            for l in range(self.nlayers):
                self.phase('A%d' % l, self.phase_A, l, x)
                self.phase('B%d' % l, self.phase_B, l)
                self.phase('C%d' % l, self.phase_C, l)
                self.phase('D%d' % l, self.phase_D, l, x)
                self.phase('E%d' % l, self.phase_E, l)
                x = self.xb[l]
            self.phase('F', self.phase_F, x)
        return self.nc


def tile_w(w, mbw):
    K, M = w.shape
    return np.ascontiguousarray(w.reshape(K // 128, 128, M // mbw, mbw).transpose(2, 1, 0, 3))


def colize(v):
    return v.reshape(-1, 128).T


def prep_shared(inp):
    f = lambda a: np.asarray(a, dtype=np.float32)
    win = np.stack([tile_w(f(inp["w_in"][l])[:, :5632], 512) for l in range(NL)])
    wdt = np.stack([np.ascontiguousarray(f(inp["w_in"][l])[:, 5632:].reshape(KC, 128, 16).transpose(1, 0, 2)) for l in range(NL)])
    wout = np.stack([tile_w(f(inp["w_out"][l]), 512) for l in range(NL)])
    w1 = np.stack([tile_w(f(inp["w_mlp_in"][l]), 512) for l in range(NL)])
    w2 = np.stack([np.ascontiguousarray(
        f(inp["w_mlp_out"][l]).reshape(2, 32, 128, 16, 128).transpose(0, 3, 2, 1, 4)) for l in range(NL)])
    cols = np.zeros((128, NL * NCOL + 16), np.float32)
    for l in range(NL):
        b = l * NCOL
        cols[:, b + 0:b + 16] = colize(f(inp["ln1_g"][l]))
        cols[:, b + 16:b + 32] = colize(f(inp["ln2_g"][l]))
        cols[:, b + 32:b + 40] = colize(f(inp["attn_norm_g"][l]))
        cols[:, b + 40:b + 48] = colize(f(inp["ssd_norm_g"][l]))
        cw = f(inp["conv_w"][l])
        for ch in range(12):
            for k in range(4):
                cols[:, b + 40 + 8 + ch * 4 + k] = cw[k, ch * 128:(ch + 1) * 128]
        cols[:, b + 96:b + 108] = colize(f(inp["conv_b"][l]))
        cols[:, b + 108:b + 116] = colize(np.repeat(f(inp["d_skip"][l]), 64))
    cols[:, NL * NCOL:] = colize(f(inp["final_norm_g"]))
    rows = np.stack([np.stack([f(inp["dt_bias"][l]), f(inp["a_log"][l])]) for l in range(NL)])
    return dict(win=win, wdt=wdt, wout=wout, w1=w1, w2=w2, cols=cols, rows=rows)


def kernel(**inputs):
    x = np.asarray(inputs["x"], dtype=np.float32)
    shared = prep_shared(inputs)
    nb = x.shape[0]
    in_maps = []
    for b in range(nb):
        m = dict(shared)
        m["xT"] = np.ascontiguousarray(x[b].T)
        in_maps.append(m)
    nc = Builder().build()
    res = run_bass_kernel_spmd(nc, in_maps, core_ids=list(range(nb)))
    out = np.stack([np.ascontiguousarray(res.results[b]["outT"].T) for b in range(nb)])
    return out.astype(np.float32)
```

```python
import contextlib
import numpy as np
import concourse.bass as bass
import concourse.mybir as mybir
from concourse.bass_utils import run_bass_kernel_spmd

F32 = mybir.dt.float32
BF16 = mybir.dt.bfloat16
I32 = mybir.dt.int32
AF = mybir.ActivationFunctionType
ALU = mybir.AluOpType

D = 2048
S_LEN = 4096
NL = 2
KC = 16
TT = 1024
NT = S_LEN // TT
EPS = 1e-5
NCOL = 116
DILS = (1, 4, 16)

COMPUTE = ('pe', 'act', 'dve', 'pool')
QUEUES = ('sp', 'act', 'pool')


class T:
    __slots__ = ('name', 'writers', 'readers', 'gen_deps', 'excl')

    def __init__(self, name='', excl=False):
        self.name = name
        self.writers = []
        self.readers = []
        self.gen_deps = []
        self.excl = excl


def TL(name, n):
    return [T('%s%d' % (name, i)) for i in range(n)]


def TX(name):
    return T(name, excl=True)


def TLX(name, n):
    return [T('%s%d' % (name, i), excl=True) for i in range(n)]


class Ins:
    __slots__ = ('eng', 'fn', 'deps', 'signal', 'sigval', 'is_dma', 'dsem', 'dval', 'prev_dma')

    def __init__(self, eng, fn, is_dma=False):
        self.eng = eng
        self.fn = fn
        self.deps = []
        self.signal = False
        self.sigval = None
        self.is_dma = is_dma
        self.dsem = None
        self.dval = None
        self.prev_dma = None


NPOOL = {'sp': 24, 'act': 8, 'pool': 12}


class SemState:
    def __init__(self, nc, st):
        self.csem = {e: st.enter_context(nc.semaphore('c_' + e)) for e in COMPUTE}
        self.dsem = {}
        for q in QUEUES:
            for j in range(NPOOL[q]):
                self.dsem[(q, j)] = st.enter_context(nc.semaphore('d_%s%d' % (q, j)))
        self.ccount = {e: 0 for e in COMPUTE}
        self.ndma = {q: 0 for q in QUEUES}


class Sched:
    def __init__(self, nc, sems, same_engine_sync=True):
        self.nc = nc
        self.sems = sems
        self.same = same_engine_sync
        self.lists = {e: [] for e in ('pe', 'act', 'dve', 'pool', 'sp')}
        self.ndma = dict(sems.ndma)
        self.ndma0 = dict(sems.ndma)
        self.npool = NPOOL
        self.dma_hist = {q: [] for q in QUEUES}

    def _deps_for(self, ins, reads, writes, partial):
        deps = []
        ex = []
        for t in list(reads) + list(writes):
            if t.excl and t not in ex:
                ex.append(t)
        reads = [t for t in reads if not t.excl]
        writes = [t for t in writes if not t.excl]
        for t in ex:
            deps.extend(t.writers)
            t.writers = [ins]
        for t in reads:
            deps.extend(t.writers)
        for t in writes:
            if partial:
                if t.readers or not t.writers:
                    t.gen_deps = list(t.readers) + list(t.writers)
                    t.writers = []
                    t.readers = []
                deps.extend(t.gen_deps)
            else:
                deps.extend(t.readers)
                deps.extend(t.writers)
        for t in reads:
            t.readers.append(ins)
        for t in writes:
            if partial:
                t.writers.append(ins)
            else:
                t.writers = [ins]
                t.readers = []
                t.gen_deps = []
        seen = set()
        out = []
        for d in deps:
            if id(d) in seen or d is ins:
                continue
            seen.add(id(d))
            out.append(d)
        ins.deps = out
        for d in out:
            if d.is_dma:
                continue
            if d.eng == ins.eng and not ins.is_dma and (ins.eng == 'pe' or not self.same):
                continue
            d.signal = True

    def op(self, eng, fn, reads=(), writes=(), partial=False):
        ins = Ins(eng, fn)
        self._deps_for(ins, reads, writes, partial)
        self.lists[eng].append(ins)
        return ins

    def dma(self, q, out, in_, reads=(), writes=(), partial=False, **kw):
        def fn(e, out=out, in_=in_, kw=kw):
            return e.dma_start(out=out, in_=in_, **kw)
        ins = Ins(q, fn, is_dma=True)
        self._deps_for(ins, reads, writes, partial)
        k = self.ndma[q]
        P = self.npool[q]
        ins.dsem = (q, k % P)
        ins.dval = 16 * (k // P + 1)
        hist = self.dma_hist[q]
        if k - self.ndma0[q] >= P:
            ins.prev_dma = hist[k - self.ndma0[q] - P]
        hist.append(ins)
        self.ndma[q] = k + 1
        self.lists[q].append(ins)
        return ins

    def emit(self):
        nc = self.nc
        with contextlib.ExitStack() as st:
            csem = self.sems.csem
            dsem = self.sems.dsem
            for e in COMPUTE:
                c = self.sems.ccount[e]
                for ins in self.lists[e]:
                    if ins.is_dma:
                        continue
                    if ins.signal:
                        c += 1
                        ins.sigval = c
                self.sems.ccount[e] = c
            for q in QUEUES:
                self.sems.ndma[q] = self.ndma[q]
            block = st.enter_context(nc.Block())
            same = self.same

            def run(engname, engobj):
                known = {}

                def wait(key, semh, val):
                    if known.get(key, 0) >= val:
                        return
                    known[key] = val
                    engobj.wait_ge(semh, val)

                for ins in self.lists[engname]:
                    for d in ins.deps:
                        if d.is_dma:
                            wait(d.dsem, dsem[d.dsem], d.dval)
                        else:
                            if d.eng == engname and not ins.is_dma:
                                if engname == 'pe' or not same:
                                    continue
                            wait(d.eng, csem[d.eng], d.sigval)
                    if ins.is_dma:
                        if ins.prev_dma is not None:
                            p = ins.prev_dma
                            wait(p.dsem, dsem[p.dsem], p.dval)
                        bi = ins.fn(engobj)
                        bi.then_inc(dsem[ins.dsem], 16)
                    else:
                        bi = ins.fn(engobj)
                        if ins.signal:
                            bi.then_inc(csem[engname], 1)
                if engname in QUEUES:
                    hist = self.dma_hist[engname]
                    P = self.npool[engname]
                    for ins in hist[-P:]:
                        wait(ins.dsem, dsem[ins.dsem], ins.dval)

            @block.tensor
            def _(e):
                run('pe', e)

            @block.scalar
            def _(e):
                run('act', e)

            @block.vector
            def _(e):
                run('dve', e)

            @block.gpsimd
            def _(e):
                run('pool', e)

            @block.sync
            def _(e):
                run('sp', e)


class Ops:
    def __init__(self, S):
        self.S = S

    def mm(self, out, lhsT, rhs, start, stop, reads, writes):
        return self.S.op('pe', lambda e: e.matmul(out, lhsT=lhsT, rhs=rhs, start=start, stop=stop),
                         reads=reads, writes=writes, partial=True)

    def tr(self, out, in_, ident, reads, writes):
        return self.S.op('pe', lambda e: e.transpose(out, in_, ident), reads=reads, writes=writes, partial=True)

    def act(self, out, in_, func, reads, writes, bias=None, scale=None, partial=False, eng='act'):
        kw = {}
        if bias is not None:
            kw['bias'] = bias
        if scale is not None:
            kw['scale'] = scale
        return self.S.op('act', lambda e: e.activation(out=out, in_=in_, func=func, **kw),
                         reads=reads, writes=writes, partial=partial)

    def tt(self, eng, out, in0, in1, op, reads, writes, partial=False):
        return self.S.op(eng, lambda e: e.tensor_tensor(out=out, in0=in0, in1=in1, op=op),
                         reads=reads, writes=writes, partial=partial)

    def ts(self, eng, out, in0, s1, op0, reads, writes, s2=None, op1=None, partial=False):
        if op1 is None:
            return self.S.op(eng, lambda e: e.tensor_scalar(out=out, in0=in0, scalar1=s1, scalar2=None, op0=op0),
                             reads=reads, writes=writes, partial=partial)
        return self.S.op(eng, lambda e: e.tensor_scalar(out=out, in0=in0, scalar1=s1, scalar2=s2, op0=op0, op1=op1),
                         reads=reads, writes=writes, partial=partial)

    def stt(self, eng, out, in0, scalar, in1, op0, op1, reads, writes, partial=False):
        return self.S.op(eng, lambda e: e.scalar_tensor_tensor(out=out, in0=in0, scalar=scalar, in1=in1, op0=op0, op1=op1),
                         reads=reads, writes=writes, partial=partial)

    def cp(self, eng, out, in_, reads, writes, partial=False):
        if eng == 'act':
            return self.S.op('act', lambda e: e.activation(out=out, in_=in_, func=AF.Copy),
                             reads=reads, writes=writes, partial=partial)
        return self.S.op(eng, lambda e: e.tensor_copy(out=out, in_=in_), reads=reads, writes=writes, partial=partial)

    def recip(self, out, in_, reads, writes, partial=False):
        return self.S.op('dve', lambda e: e.reciprocal(out=out, in_=in_), reads=reads, writes=writes, partial=partial)

    def memset(self, eng, ap, val, writes, partial=False):
        return self.S.op(eng, lambda e: e.memset(ap, val), writes=writes, partial=partial)


class Builder:
    def __init__(self, debug=(), nlayers=NL, phases=None):
        self.debug = set(debug)
        self.nlayers = nlayers
        self.phases = phases
        nc = bass.Bass("TRN2", target_bir_lowering=False)
        self.nc = nc
        dt = nc.dram_tensor

        def nlw(ph):
            if phases is None:
                return NL
            ls = [int(p[1]) for p in phases if p[0] == ph]
            return (max(ls) + 1) if ls else 0
        self.wshapes = {
            "win": [max(nlw('A'), 1), 11 if nlw('A') else 1, 128, KC, 512],
            "wdt": [max(nlw('A'), 1), 128, KC, 16],
            "wout": [max(nlw('D'), 1), 4 if nlw('D') else 1, 128, KC, 512],
            "w1": [max(nlw('E'), 1), 16 if nlw('E') else 1, 128, KC, 512],
            "w2": [max(nlw('E'), 1), 2, 16 if nlw('E') else 1, 128, 32, 128],
        }
        self.xT = dt("xT", [D, S_LEN], F32, kind="ExternalInput").ap()
        self.win = dt("win", self.wshapes["win"], F32, kind="ExternalInput").ap()
        self.wdt = dt("wdt", self.wshapes["wdt"], F32, kind="ExternalInput").ap()
        self.wout = dt("wout", self.wshapes["wout"], F32, kind="ExternalInput").ap()
        self.w1 = dt("w1", self.wshapes["w1"], F32, kind="ExternalInput").ap()
        self.w2 = dt("w2", self.wshapes["w2"], F32, kind="ExternalInput").ap()
        self.cols = dt("cols", [128, NL * NCOL + 16], F32, kind="ExternalInput").ap()
        self.rows = dt("rows", [NL, 2, 16], F32, kind="ExternalInput").ap()
        self.outT = dt("outT", [D, S_LEN], F32, kind="ExternalOutput").ap()

        def scratch(name, shape, dtype):
            kind = "ExternalOutput" if name in self.debug else "Internal"
            return dt(name, shape, dtype, kind=kind).ap()
        self.qT = [scratch("qT%d" % l, [1024, S_LEN], BF16) for l in range(NL)]
        self.kT = [scratch("kT%d" % l, [1024, S_LEN], BF16) for l in range(NL)]
        self.vT = [scratch("vT%d" % l, [1024, S_LEN], BF16) for l in range(NL)]
        self.zT = [scratch("zT%d" % l, [1024, S_LEN], F32) for l in range(NL)]
        self.xbcT = [scratch("xbcT%d" % l, [1536, S_LEN], F32) for l in range(NL)]
        self.dtr = [scratch("dtr%d" % l, [32, 128, 16], F32) for l in range(NL)]
        self.attn_raw = [scratch("attn_raw%d" % l, [1024, S_LEN], F32) for l in range(NL)]
        self.y_raw = [scratch("y_raw%d" % l, [1024, S_LEN], F32) for l in range(NL)]
        self.xa = [scratch("xa%d" % l, [D, S_LEN], F32) for l in range(NL)]
        self.xb = [scratch("xb%d" % l, [D, S_LEN], F32) for l in range(NL)]

    def phase(self, name, fn, *args):
        if self.phases is not None and name not in self.phases:
            return
        nc = self.nc
        with contextlib.ExitStack() as st:
            self.st = st
            self.pname = name
            self.S = Sched(nc, self.sems)
            self.O = Ops(self.S)
            fn(*args)
            self.S.emit()
        nc.all_engine_barrier()

    def sb(self, name, shape, dtype):
        return self.st.enter_context(self.nc.sbuf_tensor(self.pname + "_" + name, shape, dtype))

    def ps(self, name, shape, dtype=F32):
        return self.st.enter_context(self.nc.psum_tensor(self.pname + "_" + name, shape, dtype))

    def bank(self, name, dtype=F32):
        n = 512 if dtype == F32 else 1024
        return self.st.enter_context(self.nc.psum_tensor(self.pname + "_" + name, [128, n], dtype))

    def col(self, l, off, n=1):
        base = l * NCOL + off
        return self.colt[:, base:base + n]

    def load_consts(self, need_ident=False, need_tri=False):
        S, O = self.S, self.O
        self.colt = self.sb("colt", [128, NL * NCOL + 16], F32)
        self.t_col = T('colt')
        S.dma('sp', self.colt[:], self.cols, writes=[self.t_col])
        self.ones_bf = self.sb("ones_bf", [128, 128], BF16)
        self.t_ones = T('ones')
        O.memset('pool', self.ones_bf[:], 1.0, [self.t_ones])
        self.eps_c = self.sb("eps_c", [128, 1], F32)
        self.t_eps = T('eps')
        O.memset('pool', self.eps_c[:], EPS, [self.t_eps])
        if need_ident or need_tri:
            self.io_i = self.sb("io_i", [128, 256], I32)
            self.io_f = self.sb("io_f", [128, 256], F32)
            t_ii = T('io_i')
            self.t_iof = T('io_f')
            io_i = self.io_i
            S.op('pool', lambda e: e.iota(io_i[:].rearrange("p (k c) -> p k c", k=2), pattern=[[-128, 2], [1, 128]],
                                          base=128, channel_multiplier=-1), writes=[t_ii])
            O.cp('dve', self.io_f[:], self.io_i[:], [t_ii], [self.t_iof])
            self.ident = self.sb("ident", [128, 128], BF16)
            self.t_ident = T('ident')
            O.ts('dve', self.ident[:], self.io_f[:, 128:256], 0.0, ALU.is_equal, [self.t_iof], [self.t_ident])
        if need_tri:
            self.tri = self.sb("tri", [128, 128], F32)
            self.t_tri = T('tri')
            O.ts('dve', self.tri[:], self.io_f[:, 128:256], 0.0, ALU.is_ge, [self.t_iof], [self.t_tri])
            self.ones_f = self.sb("ones_f", [128, 128], F32)
            self.t_onesf = T('onesf')
            O.memset('pool', self.ones_f[:], 1.0, [self.t_onesf])
            self.one_c = self.sb("one_c", [128, 1], F32)
            self.t_onec = T('onec')
            O.memset('pool', self.one_c[:], 1.0, [self.t_onec])

    def rms_tile(self, src, nch, c0, ncols, gcol0_l, gcol_off, dst, dst_c0, t_dst, stage, t_stage, sq, t_sq, ps_stat, t_ps,
                 rstd, t_rstd, tag, out_f32_dram=None):
        S, O = self.S, self.O
        nfeat = nch * 128
        S.dma('sp', stage[:, 0:nch, :], src[:, c0:c0 + ncols].rearrange("(kc p) t -> p kc t", p=128), writes=[t_stage])
        for ch in range(nch):
            sqb = sq[ch % 2]
            O.act(sqb[:], stage[:, ch, :], AF.Square, [t_stage], [t_sq[ch % 2]])
            O.mm(ps_stat[:, 0:512], self.ones_bf[:], sqb[:], ch == 0, ch == nch - 1, [self.t_ones, t_sq[ch % 2]], [t_ps])
        O.act(rstd[:], ps_stat[:, 0:512], AF.Sqrt, [t_ps, self.t_eps], [t_rstd], bias=self.eps_c[:, 0:1], scale=1.0 / nfeat)
        O.recip(rstd[:], rstd[:], [t_rstd], [t_rstd])
        for ch in range(nch):
            eng = 'dve'
            gc = self.col(gcol0_l, gcol_off + ch)
            if out_f32_dram is None:
                O.stt(eng, dst[:, ch, dst_c0:dst_c0 + 512], stage[:, ch, :], gc, rstd[:], ALU.mult, ALU.mult,
                      [t_stage, t_rstd, self.t_col], [t_dst], partial=True)
            else:
                O.stt(eng, dst[:, ch, :], stage[:, ch, :], gc, rstd[:], ALU.mult, ALU.mult,
                      [t_stage, t_rstd, self.t_col], [t_dst], partial=True)
        if out_f32_dram is not None:
            S.dma('sp', out_f32_dram[:, c0:c0 + ncols].rearrange("(kc p) t -> p kc t", p=128), dst[:, 0:nch, :], reads=[t_dst])

    def phase_A(self, l, xin):
        S, O = self.S, self.O
        self.load_consts()
        hT = [self.sb("hT%d" % i, [128, KC, TT], BF16) for i in range(2)]
        t_hT = [[T('hT%d_%d' % (i, j)) for j in range(2)] for i in range(2)]
        xst = self.sb("xst", [128, KC, 512], F32)
        t_xst = T('xst')
        sq = [self.sb("sq%d" % i, [128, 512], BF16) for i in range(2)]
        t_sq = TL('sq', 2)
        rstd = self.sb("rstd", [128, 512], F32)
        t_rstd = T('rstd')
        NW = 3
        wb = [self.sb("wb%d" % i, [128, KC, 512], BF16) for i in range(NW)]
        t_wb = TL('wb', NW)
        wdtb = self.sb("wdtb", [128, KC, 16], BF16)
        t_wdt = T('wdt')
        NE = 4
        ev_bf = [self.sb("evb%d" % i, [128, TT], BF16) for i in range(NE)]
        ev_f = [self.sb("evf%d" % i, [128, TT], F32) for i in range(NE)]
        t_evb = TL('evb', NE)
        t_evf = TL('evf', NE)
        dts = self.sb("dts", [128, 8, 16], F32)
        t_dts = T('dts')
        ps_stat = self.bank("ps_stat")
        t_pss = TX('pss')
        NP = 4
        pmm = [self.bank("pmm%d" % i) for i in range(NP)]
        t_pmm = TLX('pmm', NP)
        ps_dt_b = self.bank("ps_dt")
        ps_dt = ps_dt_b[:, 0:128].rearrange("p (a b) -> p a b", a=8)
        t_psdt = TX('psdt')

        S.dma('pool', wdtb[:], self.wdt[l], writes=[t_wdt])

        def ln_tile(tt):
            hb = tt % 2
            for sub in range(2):
                self.rms_tile(xin, KC, tt * TT + sub * 512, 512, l, 0, hT[hb], sub * 512, t_hT[hb][sub], xst, t_xst,
                              sq, t_sq, ps_stat, t_pss, rstd, t_rstd, 'A')

        nw = 0
        wsched = [(tt, mb) for tt in range(NT) for mb in range(11)]

        def issue_w(i):
            if i < len(wsched):
                tt_, mb_ = wsched[i]
                S.dma('pool', wb[i % NW][:], self.win[l, mb_], writes=[t_wb[i % NW]])
        issue_w(0)
        issue_w(1)
        ln_tile(0)
        kps = 0
        kev = 0
        for tt in range(NT):
            hb = tt % 2
            tok0 = tt * TT
            for mb in range(11):
                wi = tt * 11 + mb
                issue_w(wi + 2)
                w = wb[wi % NW]
                tw = t_wb[wi % NW]
                if mb == 5 and tt + 1 < NT:
                    ln_tile(tt + 1)
                for s4 in range(4):
                    m = mb * 4 + s4
                    is_bf = m < 24
                    ei = kev % NE
                    kev += 1
                    ev = ev_bf[ei] if is_bf else ev_f[ei]
                    tev = t_evb[ei] if is_bf else t_evf[ei]
                    for th in range(2):
                        pi = kps % NP
                        kps += 1
                        for kc in range(KC):
                            O.mm(pmm[pi][:, 0:512], w[:, kc, s4 * 128:(s4 + 1) * 128], hT[hb][:, kc, th * 512:(th + 1) * 512],
                                 kc == 0, kc == KC - 1, [tw, t_hT[hb][th]], [t_pmm[pi]])
                        eng = 'act' if (kps % 2 == 0) else 'dve'
                        O.cp(eng, ev[:, th * 512:(th + 1) * 512], pmm[pi][:, 0:512], [t_pmm[pi]], [tev], partial=True)
                    if m < 8:
                        dst = self.qT[l][m * 128:(m + 1) * 128, tok0:tok0 + TT]
                    elif m < 16:
                        dst = self.kT[l][(m - 8) * 128:(m - 7) * 128, tok0:tok0 + TT]
                    elif m < 24:
                        dst = self.vT[l][(m - 16) * 128:(m - 15) * 128, tok0:tok0 + TT]
                    elif m < 32:
                        dst = self.zT[l][(m - 24) * 128:(m - 23) * 128, tok0:tok0 + TT]
                    else:
                        dst = self.xbcT[l][(m - 32) * 128:(m - 31) * 128, tok0:tok0 + TT]
                    S.dma('sp', dst, ev[:], reads=[tev])
            for tb in range(8):
                for kc in range(KC):
                    O.mm(ps_dt[:, tb, :], hT[hb][:, kc, tb * 128:(tb + 1) * 128], wdtb[:, kc, :], kc == 0, kc == KC - 1,
                         [t_wdt, t_hT[hb][tb // 4]], [t_psdt])
            O.cp('dve', dts[:], ps_dt, [t_psdt], [t_dts])
            S.dma('sp', self.dtr[l][tt * 8:(tt + 1) * 8].rearrange("c p h -> p c h"), dts[:], reads=[t_dts])

    def phase_B(self, l):
        S, O = self.S, self.O
        self.load_consts(need_ident=True)
        qc = [self.sb("qc%d" % i, [128, S_LEN], BF16) for i in range(2)]
        kc_ = [self.sb("kc%d" % i, [128, S_LEN], BF16) for i in range(2)]
        vc = [self.sb("vc%d" % i, [128, S_LEN], BF16) for i in range(2)]
        t_q = TL('q', 2)
        t_k = TL('k', 2)
        t_v = TL('v', 2)
        acc = self.sb("acc", [128, 2, S_LEN], F32)
        t_acc = TL('acc', 8)
        Vt = [self.sb("Vt%d" % i, [128, 32, 2, 128], BF16) for i in range(2)]
        t_Vt = [TL('Vt%d_' % i, 32) for i in range(2)]
        Et = [self.sb("Et%d" % i, [128, 2, 256], BF16) for i in range(3)]
        t_Et = TL('Et', 3)
        Dc = self.sb("Dc", [128, 256], F32)
        msk = self.sb("msk", [128, 256], F32)
        t_Dc = T('Dc')
        t_msk = T('msk')
        NPB = 3
        pexp = [self.sb("pexp%d" % i, [128, 2, 256], BF16) for i in range(NPB)]
        Pb = [self.sb("Pb%d" % i, [128, 2, 256], BF16) for i in range(NPB)]
        t_pexp = TL('pexp', NPB)
        t_Pb = TL('Pb', NPB)
        rec = self.sb("rec", [128, S_LEN], F32)
        t_rec = T('rec')
        ob = self.sb("ob", [128, S_LEN], F32)
        t_ob = T('ob')
        ps_s = [self.ps("ps_s%d" % i, [128, 2, 512]) for i in range(2)]
        t_pss = [TLX('pss%d_' % i, 2) for i in range(2)]
        ps_o_b = [self.bank("ps_o%d" % i) for i in range(2)]
        ps_o = [x[:, 0:256].rearrange("p (a b) -> p a b", a=2) for x in ps_o_b]
        t_pso = TLX('pso', 2)
        ps_t_b = [self.bank("ps_t%d" % i, BF16) for i in range(2)]
        ps_t = [x[:, 0:512].rearrange("p (a b) -> p a b", a=4) for x in ps_t_b]
        t_pst = TLX('pst', 2)

        O.ts('dve', Dc[:], self.io_f[:], 0.0, ALU.max, [self.t_iof], [t_Dc], s2=128.0, op1=ALU.min)
        O.ts('dve', msk[:], self.io_f[:], 0.0, ALU.is_ge, [self.t_iof], [t_msk])
        O.stt('dve', msk[:], self.io_f[:], 128.0, msk[:], ALU.is_le, ALU.mult, [self.t_iof, t_msk], [t_msk])
        import os
        KB = os.environ.get("KB", "")
        for i in range(2):
            if 'noms' in KB:
                break
            O.memset('pool', Vt[i][:, :, 0, 64:128], 1.0, t_Vt[i], partial=True)
            O.memset('pool', Vt[i][:, :, 1, 0:64], 1.0, t_Vt[i], partial=True)

        def load_pair(hp):
            b = hp % 2
            r0 = hp * 128
            S.dma('sp', qc[b][:], self.qT[l][r0:r0 + 128, :], writes=[t_q[b]])
            S.dma('sp', kc_[b][:], self.kT[l][r0:r0 + 128, :], writes=[t_k[b]])
            S.dma('sp', vc[b][:], self.vT[l][r0:r0 + 128, :], writes=[t_v[b]])
        load_pair(0)
        ku = 0
        kt = 0
        kvt = 0
        import os
        KB = os.environ.get("KB", "")
        nhp = int(os.environ.get("KB_HP", "8"))
        nbr = int(os.environ.get("KB_BR", "3"))
        for hp in range(nhp):
            pb_ = hp % 2
            if hp + 1 < 8:
                load_pair(hp + 1)
            Q, K, V = qc[pb_], kc_[pb_], vc[pb_]
            for b in range(3):
                if 'noE' in KB:
                    break
                for hh in range(2):
                    h = 2 * hp + hh
                    slope = 2.0 ** (-8.0 * (h + 1) / 16.0)
                    O.act(Et[b][:, hh, :], Dc[:], AF.Exp, [t_Dc], [t_Et[b]], scale=-slope * DILS[b], partial=True)
                O.tt('dve', Et[b][:], Et[b][:], msk[:].unsqueeze(1).to_broadcast([128, 2, 256]), ALU.mult,
                     [t_Et[b], t_msk], [t_Et[b]])
            for b in range(nbr):
                dil = DILS[b]
                nblk = 32 // dil
                vb = kvt % 2
                kvt += 1
                VT = Vt[vb]
                tVT = t_Vt[vb]

                def tslice(r, n):
                    base = r + dil * 128 * n
                    return slice(base, base + dil * 127 + 1, dil)
                for kb0 in range(0, 32, 4):
                    if 'noV' in KB:
                        break
                    if 'V1' in KB and kb0 >= 4:
                        break
                    if 'V2' in KB and kb0 >= 8:
                        break
                    if 'V3' in KB and kb0 >= 12:
                        break
                    pt = ps_t[kt % 2]
                    tpt = t_pst[kt % 2]
                    kt += 1
                    for i4 in range(4):
                        kb = kb0 + i4
                        r, n = kb // nblk, kb % nblk
                        O.tr(pt[:, i4, :], V[:, tslice(r, n)], self.ident[:], [t_v[pb_], self.t_ident], [tpt])
                    if 'nocp' in KB:
                        continue
                    O.cp('act', VT[:, kb0:kb0 + 4, 0, 0:64], pt[:, :, 0:64], [tpt], tVT[kb0:kb0 + 4], partial=True)
                    O.cp('dve', VT[:, kb0:kb0 + 4, 1, 64:128], pt[:, :, 64:128], [tpt], tVT[kb0:kb0 + 4], partial=True)
                ust = {}

                def emit_qk(kb):
                    nonlocal ku
                    r, n = kb // nblk, kb % nblk
                    qs = tslice(r, n)
                    si = ku % 2
                    pi = ku % NPB
                    ku += 1
                    pss, tps = ps_s[si], t_pss[si]
                    lo = 0 if n > 0 else 128
                    for hh in range(2):
                        pr = slice(hh * 64, hh * 64 + 64)
                        if n > 0:
                            O.mm(pss[:, hh, 0:128], K[pr, tslice(r, n - 1)], Q[pr, qs], True, True, [t_k[pb_], t_q[pb_]], [tps[hh]])
                        O.mm(pss[:, hh, 128:256], K[pr, qs], Q[pr, qs], True, True, [t_k[pb_], t_q[pb_]], [tps[hh]])
                    O.act(pexp[pi][:, :, lo:256], pss[:, :, lo:256], AF.Exp, tps, [t_pexp[pi]], scale=0.125)
                    meng = 'dve'
                    O.tt(meng, Pb[pi][:, :, lo:256], pexp[pi][:, :, lo:256], Et[b][:, :, lo:256], ALU.mult,
                         [t_pexp[pi], t_Et[b]], [t_Pb[pi]])
                    ust[kb] = (r, n, qs, si, pi)

                def emit_pv(kb):
                    r, n, qs, si, pi = ust.pop(kb)
                    pso, tpo = ps_o[si], t_pso[si]
                    for hh in range(2):
                        if n > 0:
                            O.mm(pso[:, hh, :], VT[:, kb - 1, hh, :], Pb[pi][:, hh, 0:128], True, False, [tVT[kb - 1], t_Pb[pi]], [tpo])
                        O.mm(pso[:, hh, :], VT[:, kb, hh, :], Pb[pi][:, hh, 128:256], n == 0, True, [tVT[kb], t_Pb[pi]], [tpo])
                    base = r + dil * 128 * n
                    blks = list(range(base // 512, (base + dil * 127) // 512 + 1))
                    tacc = [t_acc[x] for x in blks]
                    if b == 0:
                        O.cp('act', acc[:, :, qs], pso, [tpo], tacc, partial=True)
                    else:
                        O.tt('dve', acc[:, :, qs], acc[:, :, qs], pso, ALU.add, [tpo] + tacc, tacc)

                emit_qk(0)
                for kb in range(32):
                    if kb + 1 < 32:
                        emit_qk(kb + 1)
                    emit_pv(kb)
            if 'nofin' in KB:
                continue
            O.recip(rec[0:64, :], acc[64:128, 0, :], t_acc, [t_rec], partial=True)
            O.recip(rec[64:128, :], acc[0:64, 1, :], t_acc, [t_rec], partial=True)
            O.tt('pool', ob[0:64, :], acc[0:64, 0, :], rec[0:64, :], ALU.mult, t_acc + [t_rec], [t_ob], partial=True)
            O.tt('pool', ob[64:128, :], acc[64:128, 1, :], rec[64:128, :], ALU.mult, t_acc + [t_rec], [t_ob], partial=True)
            S.dma('sp', self.attn_raw[l][hp * 128:(hp + 1) * 128, :], ob[:], reads=[t_ob])

    def phase_C(self, l):
        S, O = self.S, self.O
        self.load_consts(need_ident=True, need_tri=True)
        tri, ones_f = self.tri, self.ones_f
        dtb = self.sb("dtb", [128, 16], F32)
        aneg = self.sb("aneg", [128, 16], F32)
        t_dtb, t_an = T('dtb'), T('aneg')
        S.dma('sp', dtb[:], self.rows[l, 0].partition_broadcast(128), writes=[t_dtb])
        S.dma('sp', aneg[:], self.rows[l, 1].partition_broadcast(128), writes=[t_an])
        O.act(aneg[:], aneg[:], AF.Exp, [t_an], [t_an])
        O.ts('dve', aneg[:], aneg[:], -1.0, ALU.mult, [t_an], [t_an])
        dt_all = self.sb("dt_all", [128, 32, 16], F32)
        dA_all = self.sb("dA_all", [128, 32, 16], F32)
        t_dt, t_dA = T('dt'), T('dA')
        S.dma('sp', dt_all[:], self.dtr[l].rearrange("c p h -> p c h"), writes=[t_dt])
        O.tt('dve', dt_all[:], dt_all[:], dtb[:].unsqueeze(1).to_broadcast([128, 32, 16]), ALU.add, [t_dt, t_dtb], [t_dt])
        O.act(dt_all[:], dt_all[:], AF.Exp, [t_dt], [t_dt])
        O.act(dt_all[:], dt_all[:], AF.Ln, [t_dt, self.t_onec], [t_dt], bias=self.one_c[:, 0:1])
        O.tt('dve', dA_all[:], dt_all[:], aneg[:].unsqueeze(1).to_broadcast([128, 32, 16]), ALU.mult, [t_dt, t_an], [t_dA])

        xr = self.sb("xr", [128, 12, 515], F32)
        t_xr = T('xr')
        cv = self.sb("cv", [128, 12, 512], F32)
        t_cv = TL('cv', 12)
        xs_bf = self.sb("xs_bf", [128, 8, 512], BF16)
        t_xsb = T('xsb')
        bcf = self.sb("bcf", [128, 4, 512], BF16)
        t_bcf = T('bcf')
        zt = self.sb("zt", [128, 8, 512], F32)
        t_zt = T('zt')
        yst = self.sb("yst", [128, 8, 512], F32)
        t_yst = T('yst')
        def mk2(name, shape, dt_):
            return [self.sb("%s_%d" % (name, i), shape, dt_) for i in range(2)]
        cbm2, t_cbm2 = mk2("cbm", [128, 2, 128], BF16), TL('cbm', 2)
        Btok2, t_Btok2 = mk2("Btok", [128, 2, 128], BF16), TL('Btok', 2)
        Xdt2, t_Xdt2 = mk2("Xdt", [128, 16, 64], BF16), TL('Xdt', 2)
        Xw2, t_Xw2 = mk2("Xw", [128, 16, 64], BF16), TL('Xw', 2)
        nacs2, t_nacs2 = mk2("nacs", [128, 16], F32), TL('nacs', 2)
        dAtri2, t_dAtri2 = mk2("dAtri", [128, 16, 128], F32), [TL('dAtri%d_' % i, 4) for i in range(2)]
        E22, t_E22 = mk2("E2", [128, 16, 128], F32), [TL('E2%d_' % i, 4) for i in range(2)]
        Lraw2, t_Lraw2 = mk2("Lraw", [128, 16, 128], F32), [TL('Lraw%d_' % i, 16) for i in range(2)]
        MT2, t_MT2 = mk2("MT", [128, 16, 128], BF16), [TL('MT%d_' % i, 16) for i in range(2)]
        Cp2, t_Cp2 = mk2("Cp", [128, 16, 128], BF16), [TL('Cp%d_' % i, 4) for i in range(2)]
        tmp = self.sb("tmp", [128, 8, 128], F32)
        t_tmp = TL('tmp', 2)
        R = self.sb("R", [128, 16, 64], F32)
        t_R = T('R')
        Rbf = self.sb("Rbf", [128, 16, 64], BF16)
        t_Rbf = T('Rbf')

        ps_misc = self.bank("ps_misc")
        ps_cb = ps_misc[:, 0:256].rearrange("p (g l) -> p g l", g=2)
        ps_a = ps_misc[:, 256:272]
        t_pcb = TX('pmisc')
        t_pa = t_pcb
        ps_tx_b = self.bank("ps_tx", BF16)
        ps_tx = ps_tx_b[:, :].rearrange("p (j c) -> p j c", j=8)
        t_ptx = TX('ptx')
        ps_tb_b = self.bank("ps_tb", BF16)
        ps_tb = ps_tb_b[:, 0:256].rearrange("p (j c) -> p j c", j=2)
        t_ptb = TX('ptb')
        ps_bc_b = [self.bank("ps_bc%d" % i) for i in range(2)]
        ps_bc = [x[:, :].rearrange("p (h l) -> p h l", h=4) for x in ps_bc_b]
        t_pbc = TLX('pbc', 2)
        ps_y = self.ps("ps_y", [128, 8, 128])
        t_py = TLX('py', 2)
        ps_st = self.bank("ps_st")
        t_pst = TX('pst')

        O.memset('pool', xr[:, :, 0:3], 0.0, [t_xr])
        cw0 = 48
        kbc = 0
        for t8 in range(8):
            c0 = t8 * 512
            if t8 == 0:
                S.dma('sp', xr[:, :, 3:515], self.xbcT[l][:, 0:512].rearrange("(c p) t -> p c t", p=128), writes=[t_xr], partial=True)
            else:
                S.dma('sp', xr[:, :, :], self.xbcT[l][:, c0 - 3:c0 + 512].rearrange("(c p) t -> p c t", p=128), writes=[t_xr])
            S.dma('sp', zt[:], self.zT[l][:, c0:c0 + 512].rearrange("(c p) t -> p c t", p=128), writes=[t_zt])
            for ch in range(12):
                eng = 'dve'
                O.ts(eng, cv[:, ch, :], xr[:, ch, 0:512], self.col(l, cw0 + ch * 4 + 0), ALU.mult, [t_xr, self.t_col], [t_cv[ch]],
                     s2=self.col(l, 96 + ch), op1=ALU.add)
                for k in range(1, 4):
                    O.stt(eng, cv[:, ch, :], xr[:, ch, k:k + 512], self.col(l, cw0 + ch * 4 + k), cv[:, ch, :], ALU.mult, ALU.add,
                          [t_xr, t_cv[ch], self.t_col], [t_cv[ch]])
            O.act(cv[:, 0:8, :], cv[:, 0:8, :], AF.Silu, t_cv[0:8], t_cv[0:8])
            O.act(bcf[:], cv[:, 8:12, :], AF.Silu, t_cv[8:12], [t_bcf])
            O.cp('act', xs_bf[:], cv[:, 0:8, :], t_cv[0:8], [t_xsb])
            O.act(zt[:], zt[:], AF.Silu, [t_zt], [t_zt])
            for j in range(8):
                O.act(cv[:, j, :], cv[:, j, :], AF.Copy, [t_cv[j], self.t_col], [t_cv[j]], scale=self.col(l, 108 + j))
            def stage1(c, cc):
                nonlocal kbc
                bi = cc % 2
                cbm, t_cbm, Btok, t_Btok = cbm2[bi], t_cbm2[bi], Btok2[bi], t_Btok2[bi]
                Xdt, t_Xdt, Xw, t_Xw = Xdt2[bi], t_Xdt2[bi], Xw2[bi], t_Xw2[bi]
                nacs, t_nacs = nacs2[bi], t_nacs2[bi]
                dAtri, t_dAtri, E2, t_E2 = dAtri2[bi], t_dAtri2[bi], E22[bi], t_E22[bi]
                Lraw, t_Lraw, MT, t_MT, Cp, t_Cp = Lraw2[bi], t_Lraw2[bi], MT2[bi], t_MT2[bi], Cp2[bi], t_Cp2[bi]
                cs = slice(c * 128, (c + 1) * 128)
                for g in range(2):
                    O.mm(ps_cb[:, g, :], bcf[:, g, cs], bcf[:, 2 + g, cs], True, True, [t_bcf], [t_pcb])
                O.tt('dve', cbm[:], ps_cb, tri[:].unsqueeze(1).to_broadcast([128, 2, 128]), ALU.mult, [t_pcb, self.t_tri], [t_cbm])
                for g in range(2):
                    O.tr(ps_tb[:, g, :], bcf[:, g, cs], self.ident[:], [t_bcf, self.t_ident], [t_ptb])
                O.cp('act', Btok[:], ps_tb, [t_ptb], [t_Btok])
                for j in range(8):
                    O.tr(ps_tx[:, j, :], xs_bf[:, j, cs], self.ident[:], [t_xsb, self.t_ident], [t_ptx])
                O.tt('dve', Xdt[:], ps_tx_b[:, :].rearrange("p (h d) -> p h d", h=16),
                     dt_all[:, cc, :].unsqueeze(2).to_broadcast([128, 16, 64]), ALU.mult, [t_ptx, t_dt], [t_Xdt])
                O.mm(ps_a, tri[:], dA_all[:, cc, :], True, True, [self.t_tri, t_dA], [t_pa])
                O.ts('dve', nacs[:], ps_a, -1.0, ALU.mult, [t_pa], [t_nacs])
                for bq in range(4):
                    hs = slice(4 * bq, 4 * bq + 4)
                    g = bq // 2
                    for hl in range(4):
                        h = 4 * bq + hl
                        O.act(dAtri[:, h, :], tri[:], AF.Copy, [self.t_tri, t_dA], [t_dAtri[bq]], scale=dA_all[:, cc, h:h + 1],
                              partial=True)
                    pbc, tpbc, pbcb = ps_bc[kbc % 2], t_pbc[kbc % 2], ps_bc_b[kbc % 2]
                    kbc += 1
                    O.mm(pbcb[:, :], ones_f[:], dAtri[:, hs, :].rearrange("p h l -> p (h l)"), True, True,
                         [self.t_onesf, t_dAtri[bq]], [tpbc])
                    O.act(E2[:, hs, :], pbc, AF.Exp, [tpbc], [t_E2[bq]])
                    for hl in range(4):
                        h = 4 * bq + hl
                        O.ts('dve', Lraw[:, h, :], pbc[:, hl, :], nacs[:, h:h + 1], ALU.add, [tpbc, t_nacs], [t_Lraw[h]],
                             s2=0.0, op1=ALU.min)
                    O.act(Lraw[:, hs, :], Lraw[:, hs, :], AF.Exp, t_Lraw[4 * bq:4 * bq + 4], t_Lraw[4 * bq:4 * bq + 4])
                    O.tt('pool', MT[:, hs, :], Lraw[:, hs, :], cbm[:, g, :].unsqueeze(1).to_broadcast([128, 4, 128]), ALU.mult,
                         t_Lraw[4 * bq:4 * bq + 4] + [t_cbm], t_MT[4 * bq:4 * bq + 4])
                    O.tt('pool', Cp[:, hs, :], E2[:, hs, :], bcf[:, 2 + g, cs].unsqueeze(1).to_broadcast([128, 4, 128]), ALU.mult,
                         [t_E2[bq], t_bcf], [t_Cp[bq]])
                for h in range(16):
                    O.act(Xw[:, h, :], Xdt[:, h, :], AF.Copy, [t_Xdt, t_Lraw[h]], [t_Xw], scale=Lraw[:, h, 127:128], partial=True)

            def stage2(c, cc):
                bi = cc % 2
                Btok, t_Btok = Btok2[bi], t_Btok2[bi]
                Xdt, t_Xdt, Xw, t_Xw = Xdt2[bi], t_Xdt2[bi], Xw2[bi], t_Xw2[bi]
                E2, t_E2 = E22[bi], t_E22[bi]
                MT, t_MT, Cp, t_Cp = MT2[bi], t_MT2[bi], Cp2[bi], t_Cp2[bi]
                cs = slice(c * 128, (c + 1) * 128)
                for j in range(8):
                    for hh in range(2):
                        h = 2 * j + hh
                        o = ps_y[hh * 64:(hh + 1) * 64, j, :]
                        O.mm(o, Xdt[:, h, :], MT[:, h, :], True, cc == 0, [t_Xdt, t_MT[h]], [t_py[j // 4]])
                        if cc > 0:
                            O.mm(o, Rbf[:, h, :], Cp[:, h, :], False, True, [t_Rbf, t_Cp[h // 4]], [t_py[j // 4]])
                for yb in range(2):
                    js = slice(4 * yb, 4 * yb + 4)
                    O.tt('dve', tmp[:, js, :], ps_y[:, js, :], cv[:, js, cs], ALU.add, [t_py[yb]] + t_cv[4 * yb:4 * yb + 4], [t_tmp[yb]])
                    O.tt('pool', yst[:, js, cs], tmp[:, js, :], zt[:, js, cs], ALU.mult, [t_tmp[yb], t_zt], [t_yst], partial=True)
                if cc < 31:
                    for g in range(2):
                        gs = slice(8 * g, 8 * g + 8)
                        O.mm(ps_st[:, :], Btok[:, g, :], Xw[:, gs, :].rearrange("p h d -> p (h d)"), True, True,
                             [t_Btok, t_Xw], [t_pst])
                        stv = ps_st[:, :].rearrange("p (h d) -> p h d", h=8)
                        if cc == 0:
                            O.cp('dve', R[:, gs, :], stv, [t_pst], [t_R], partial=True)
                        else:
                            O.tt('dve', R[:, gs, :], R[:, gs, :], E2[:, gs, 127:128].to_broadcast([128, 8, 64]), ALU.mult,
                                 [t_R] + t_E2, [t_R])
                            O.tt('dve', R[:, gs, :], R[:, gs, :], stv, ALU.add, [t_R, t_pst], [t_R])
                    O.cp('act', Rbf[:], R[:], [t_R], [t_Rbf])

            stage1(0, t8 * 4)
            for c in range(4):
                if c + 1 < 4:
                    stage1(c + 1, t8 * 4 + c + 1)
                stage2(c, t8 * 4 + c)
            S.dma('sp', self.y_raw[l][:, c0:c0 + 512].rearrange("(c p) t -> p c t", p=128), yst[:], reads=[t_yst])

    def phase_D(self, l, xin):
        S, O = self.S, self.O
        self.load_consts()
        mixT = [self.sb("mixT%d" % i, [128, KC, TT], BF16) for i in range(2)]
        t_mix = [[T('mix%d_%d' % (i, j)) for j in range(2)] for i in range(2)]
        stg = self.sb("stg", [128, 8, 512], F32)
        t_stg = T('stg')
        sq = [self.sb("sq%d" % i, [128, 512], BF16) for i in range(2)]
        t_sq = TL('sq', 2)
        rstd = self.sb("rstd", [128, 512], F32)
        t_rstd = T('rstd')
        NW = 3
        wb = [self.sb("wb%d" % i, [128, KC, 512], BF16) for i in range(NW)]
        t_wb = TL('wb', NW)
        NE = 4
        xres = [self.sb("xres%d" % i, [128, TT], F32) for i in range(NE)]
        t_xres = TL('xres', NE)
        ps_stat = self.bank("ps_stat")
        t_pss = TX('pss')
        NP = 4
        pmm = [self.bank("pmm%d" % i) for i in range(NP)]
        t_pmm = TLX('pmm', NP)

        class View:
            def __init__(self, t, off):
                self.t, self.off = t, off

            def __getitem__(self, key):
                p, ch, c = key
                return self.t[p, ch + self.off, c]

        def norm_tile(tt):
            hb = tt % 2
            for sub in range(2):
                c0 = tt * TT + sub * 512
                self.rms_tile(self.attn_raw[l], 8, c0, 512, l, 32, View(mixT[hb], 0), sub * 512, t_mix[hb][sub], stg, t_stg,
                              sq, t_sq, ps_stat, t_pss, rstd, t_rstd, 'D')
                for g in range(2):
                    self.rms_tile(self.y_raw[l][g * 512:(g + 1) * 512, :], 4, c0, 512, l, 40 + 4 * g, View(mixT[hb], 8 + 4 * g),
                                  sub * 512, t_mix[hb][sub], stg, t_stg, sq, t_sq, ps_stat, t_pss, rstd, t_rstd, 'D')

        wsched = [(tt, mb) for tt in range(NT) for mb in range(4)]

        def issue_w(i):
            if i < len(wsched):
                tt_, mb_ = wsched[i]
                S.dma('pool', wb[i % NW][:], self.wout[l, mb_], writes=[t_wb[i % NW]])
        issue_w(0)
        issue_w(1)
        norm_tile(0)
        kps = 0
        kev = 0
        for tt in range(NT):
            hb = tt % 2
            tok0 = tt * TT
            for mb in range(4):
                wi = tt * 4 + mb
                issue_w(wi + 2)
                w, tw = wb[wi % NW], t_wb[wi % NW]
                if mb == 1 and tt + 1 < NT:
                    norm_tile(tt + 1)
                for s4 in range(4):
                    m = mb * 4 + s4
                    ei = kev % NE
                    kev += 1
                    xr_, txr = xres[ei], t_xres[ei]
                    S.dma('sp', xr_[:], xin[m * 128:(m + 1) * 128, tok0:tok0 + TT], writes=[txr])
                    for th in range(2):
                        pi = kps % NP
                        kps += 1
                        for kc in range(KC):
                            O.mm(pmm[pi][:, 0:512], w[:, kc, s4 * 128:(s4 + 1) * 128], mixT[hb][:, kc, th * 512:(th + 1) * 512],
                                 kc == 0, kc == KC - 1, [tw, t_mix[hb][th]], [t_pmm[pi]])
                        O.tt('dve', xr_[:, th * 512:(th + 1) * 512], xr_[:, th * 512:(th + 1) * 512], pmm[pi][:, 0:512], ALU.add,
                             [t_pmm[pi], txr], [txr])
                    S.dma('sp', self.xa[l][m * 128:(m + 1) * 128, tok0:tok0 + TT], xr_[:], reads=[txr])

    def phase_E(self, l):
        S, O = self.S, self.O
        self.load_consts()
        xa, xb = self.xa[l], self.xb[l]
        hT = self.sb("h2T", [128, KC, TT], BF16)
        t_hT = TL('h2T', 2)
        aT = self.sb("aT", [128, 32, TT], BF16)
        t_aT = [[T('aT%d_%d' % (i, j)) for j in range(2)] for i in range(32)]
        xst = self.sb("xst", [128, KC, 512], F32)
        t_xst = T('xst')
        sq = [self.sb("sq%d" % i, [128, 512], BF16) for i in range(2)]
        t_sq = TL('sq', 2)
        rstd = self.sb("rstd", [128, 512], F32)
        t_rstd = T('rstd')
        NW = 3
        wb = [self.sb("wb%d" % i, [128, 8192], BF16) for i in range(NW)]
        t_wb = TL('wb', NW)
        NR = 3
        rl = [self.sb("rl%d" % i, [128, 512], F32) for i in range(NR)]
        t_rl = TL('rl', NR)
        NE = 4
        xres = [self.sb("xres%d" % i, [128, TT], F32) for i in range(NE)]
        t_xres = TL('xres', NE)
        ps_stat = self.bank("ps_stat")
        t_pss = TX('pss')
        NP = 6
        pmm = [self.bank("pmm%d" % i) for i in range(NP)]
        t_pmm = TLX('pmm', NP)

        wsched = []
        for tt in range(NT):
            for half in range(2):
                for b in range(8):
                    wsched.append(('w1', half * 8 + b))
                for j in range(16):
                    wsched.append(('w2', half, j))

        def issue_w(i):
            if i < len(wsched):
                e = wsched[i]
                buf = wb[i % NW]
                if e[0] == 'w1':
                    S.dma('pool', buf[:].rearrange("p (k c) -> p k c", k=KC), self.w1[l, e[1]], writes=[t_wb[i % NW]])
                else:
                    S.dma('pool', buf[:, 0:4096].rearrange("p (k c) -> p k c", k=32), self.w2[l, e[1], e[2]], writes=[t_wb[i % NW]])
        issue_w(0)
        issue_w(1)
        wi = 0
        kps = 0
        kev = 0
        krl = 0
        for tt in range(NT):
            tok0 = tt * TT
            for sub in range(2):
                self.rms_tile(xa, KC, tok0 + sub * 512, 512, l, 16, hT, sub * 512, t_hT[sub], xst, t_xst, sq, t_sq, ps_stat, t_pss,
                              rstd, t_rstd, 'E')
            for half in range(2):
                for b in range(8):
                    issue_w(wi + 2)
                    w = wb[wi % NW][:].rearrange("p (k c) -> p k c", k=KC)
                    tw = t_wb[wi % NW]
                    wi += 1
                    for s4 in range(4):
                        fc = b * 4 + s4
                        for th in range(2):
                            pi = kps % NP
                            kps += 1
                            for kc in range(KC):
                                O.mm(pmm[pi][:, 0:512], w[:, kc, s4 * 128:(s4 + 1) * 128], hT[:, kc, th * 512:(th + 1) * 512],
                                     kc == 0, kc == KC - 1, [tw, t_hT[th]], [t_pmm[pi]])
                            ri = krl % NR
                            krl += 1
                            O.act(rl[ri][:], pmm[pi][:, 0:512], AF.Relu, [t_pmm[pi]], [t_rl[ri]])
                            eng = 'dve' if krl % 2 == 0 else 'pool'
                            O.tt(eng, aT[:, fc, th * 512:(th + 1) * 512], rl[ri][:], rl[ri][:], ALU.mult, [t_rl[ri]], [t_aT[fc][th]])
                for j in range(16):
                    issue_w(wi + 2)
                    w = wb[wi % NW][:, 0:4096].rearrange("p (k c) -> p k c", k=32)
                    tw = t_wb[wi % NW]
                    wi += 1
                    ei = kev % NE
                    kev += 1
                    xr_, txr = xres[ei], t_xres[ei]
                    src = xa if half == 0 else xb
                    S.dma('sp', xr_[:], src[j * 128:(j + 1) * 128, tok0:tok0 + TT], reads=[], writes=[txr])
                    for th in range(2):
                        pi = kps % NP
                        kps += 1
                        for kc in range(32):
                            O.mm(pmm[pi][:, 0:512], w[:, kc, :], aT[:, kc, th * 512:(th + 1) * 512], kc == 0, kc == 31,
                                 [tw, t_aT[kc][th]], [t_pmm[pi]])
                        O.tt('dve', xr_[:, th * 512:(th + 1) * 512], xr_[:, th * 512:(th + 1) * 512], pmm[pi][:, 0:512], ALU.add,
                             [t_pmm[pi], txr], [txr])
                    S.dma('sp', xb[j * 128:(j + 1) * 128, tok0:tok0 + TT], xr_[:], reads=[txr])

    def phase_F(self, xin):
        S, O = self.S, self.O
        self.load_consts()
        xst = [self.sb("xst%d" % i, [128, KC, 512], F32) for i in range(2)]
        t_xst = TL('xst', 2)
        sq = [self.sb("sq%d" % i, [128, 512], BF16) for i in range(2)]
        t_sq = TL('sq', 2)
        rstd = self.sb("rstd", [128, 512], F32)
        t_rstd = T('rstd')
        ps_stat = [self.bank("ps_stat%d" % i) for i in range(2)]
        t_pss = TLX('pss', 2)
        ost = [self.sb("ost%d" % i, [128, KC, 512], F32) for i in range(2)]
        t_ost = TL('ost', 2)
        for i in range(8):
            self.rms_tile(xin, KC, i * 512, 512, NL, 0, ost[i % 2], 0, t_ost[i % 2], xst[i % 2], t_xst[i % 2], sq, t_sq,
                          ps_stat[i % 2], t_pss[i % 2], rstd, t_rstd, 'F', out_f32_dram=self.outT)

    def build(self):
        x = self.xT
        with contextlib.ExitStack() as gst:
            self.sems = SemState(self.nc, gst)
            for l in range(self.nlayers):
                self.phase('A%d' % l, self.phase_A, l, x)
                self.phase('B%d' % l, self.phase_B, l)
                self.phase('C%d' % l, self.phase_C, l)
                self.phase('D%d' % l, self.phase_D, l, x)
                self.phase('E%d' % l, self.phase_E, l)
                x = self.xb[l]
            self.phase('F', self.phase_F, x)
        return self.nc


def tile_w(w, mbw):
    K, M = w.shape
    return np.ascontiguousarray(w.reshape(K // 128, 128, M // mbw, mbw).transpose(2, 1, 0, 3))


def colize(v):
    return v.reshape(-1, 128).T


def prep_shared(inp):
    f = lambda a: np.asarray(a, dtype=np.float32)
    win = np.stack([tile_w(f(inp["w_in"][l])[:, :5632], 512) for l in range(NL)])
    wdt = np.stack([np.ascontiguousarray(f(inp["w_in"][l])[:, 5632:].reshape(KC, 128, 16).transpose(1, 0, 2)) for l in range(NL)])
    wout = np.stack([tile_w(f(inp["w_out"][l]), 512) for l in range(NL)])
    w1 = np.stack([tile_w(f(inp["w_mlp_in"][l]), 512) for l in range(NL)])
    w2 = np.stack([np.ascontiguousarray(
        f(inp["w_mlp_out"][l]).reshape(2, 32, 128, 16, 128).transpose(0, 3, 2, 1, 4)) for l in range(NL)])
    cols = np.zeros((128, NL * NCOL + 16), np.float32)
    for l in range(NL):
        b = l * NCOL
        cols[:, b + 0:b + 16] = colize(f(inp["ln1_g"][l]))
        cols[:, b + 16:b + 32] = colize(f(inp["ln2_g"][l]))
        cols[:, b + 32:b + 40] = colize(f(inp["attn_norm_g"][l]))
        cols[:, b + 40:b + 48] = colize(f(inp["ssd_norm_g"][l]))
        cw = f(inp["conv_w"][l])
        for ch in range(12):
            for k in range(4):
                cols[:, b + 40 + 8 + ch * 4 + k] = cw[k, ch * 128:(ch + 1) * 128]
        cols[:, b + 96:b + 108] = colize(f(inp["conv_b"][l]))
        cols[:, b + 108:b + 116] = colize(np.repeat(f(inp["d_skip"][l]), 64))
    cols[:, NL * NCOL:] = colize(f(inp["final_norm_g"]))
    rows = np.stack([np.stack([f(inp["dt_bias"][l]), f(inp["a_log"][l])]) for l in range(NL)])
    return dict(win=win, wdt=wdt, wout=wout, w1=w1, w2=w2, cols=cols, rows=rows)


def kernel(**inputs):
    x = np.asarray(inputs["x"], dtype=np.float32)
    shared = prep_shared(inputs)
    nb = x.shape[0]
    in_maps = []
    for b in range(nb):
        m = dict(shared)
        m["xT"] = np.ascontiguousarray(x[b].T)
        in_maps.append(m)
    nc = Builder().build()
    res = run_bass_kernel_spmd(nc, in_maps, core_ids=list(range(nb)))
    out = np.stack([np.ascontiguousarray(res.results[b]["outT"].T) for b in range(nb)])
    return out.astype(np.float32)
```
